# Optimizing a Trainium2 kernel written in Bass

```python
import math
import numpy as np
import jax, jax.numpy as jnp
from jax import lax

D_MODEL = 1024
BATCH = 8
SEQ = 2048
DEPTH = 2

GRID_W = 64
CTX_LEN = 256
CHUNK = 64
CONV_K = 5
EPS = 1e-6

SSD_HEADS = 16
SSD_HEAD_DIM = 64
SSD_WIDTH = SSD_HEADS * SSD_HEAD_DIM
SSD_GROUPS = 2
SSD_STATE = 64
SSD_BC = SSD_GROUPS * SSD_STATE
SSD_XBC = SSD_WIDTH + 2 * SSD_BC
DT_MIN = 1e-3
DT_MAX = 1e-1
ML_HEADS = 4
ML_QK_DIM = 128
ML_V_DIM = 256
ML_QK_WIDTH = ML_HEADS * ML_QK_DIM
ML_V_WIDTH = ML_HEADS * ML_V_DIM
GLA_HEADS = 4
GLA_K_DIM = 128
GLA_V_DIM = 256
GLA_K_WIDTH = GLA_HEADS * GLA_K_DIM
GLA_V_WIDTH = GLA_HEADS * GLA_V_DIM
GLA_RANK = 16
GLA_TAU = 16.0
D_FF = ((-(-8 * D_MODEL // 3) + 255) // 256) * 256

IN_SPLITS = (
    SSD_WIDTH, SSD_WIDTH, SSD_BC, SSD_BC, SSD_HEADS, SSD_HEADS,
    ML_QK_WIDTH, ML_QK_WIDTH, ML_V_WIDTH, ML_V_WIDTH,
    ML_HEADS, ML_HEADS, ML_HEADS, ML_HEADS,
    GLA_K_WIDTH, GLA_K_WIDTH, GLA_V_WIDTH, GLA_V_WIDTH, GLA_RANK, GLA_RANK,
    D_MODEL, D_MODEL, D_MODEL,
)
D_IN = sum(IN_SPLITS)

kernel_name = 'hybrid_ssd_mlstm_gla_prefix_block'


def _rmsnorm(x, w, groups=1):
    shape = x.shape
    xf = x.astype(jnp.float32).reshape(*shape[:-1], groups, shape[-1] // groups)
    xf = xf * lax.rsqrt(jnp.mean(xf * xf, axis=-1, keepdims=True) + EPS)
    return (xf.reshape(shape) * w.astype(jnp.float32)).astype(x.dtype)


def _dwconv(u, w, b):
    y = lax.conv_general_dilated(u, w[:, None, :], window_strides=(1,), padding='SAME',
                                 dimension_numbers=('NWC', 'WIO', 'NWC'),
                                 feature_group_count=u.shape[-1])
    return y + b


def _conv_two(u, n_ctx, w, b):
    return jnp.concatenate([_dwconv(u[:, :n_ctx], w, b), _dwconv(u[:, n_ctx:], w, b)], axis=1)


def _take(t, idx):
    return jnp.take(t, idx, axis=1)


def _to_chunks(t):
    bsz, T = t.shape[:2]
    return jnp.moveaxis(t.reshape(bsz, T // CHUNK, CHUNK, *t.shape[2:]), 1, 0)


def _from_chunks(y):
    nc, bsz, L = y.shape[:3]
    return jnp.moveaxis(y, 0, 1).reshape(bsz, nc * L, *y.shape[3:])


def _lower_tri():
    return jnp.tril(jnp.ones((CHUNK, CHUNK), dtype=bool))


def _ssd_scan(xh, dt, bm, cm, a_log):
    dtype = xh.dtype
    f32 = jnp.float32
    bsz, T, H, P = xh.shape
    la = dt.astype(f32) * -jnp.exp(a_log.astype(f32))
    xs = tuple(_to_chunks(t) for t in ((xh * dt[..., None]).astype(f32), la, bm.astype(f32), cm.astype(f32)))
    mask = _lower_tri()

    def step(S, inp):
        xc, lac, bc, cc = inp
        cum = jnp.cumsum(lac, axis=1)
        seg = jnp.where(mask[None, :, :, None], cum[:, :, None] - cum[:, None], -jnp.inf)
        scores = jnp.einsum('bthn,bshn->btsh', cc, bc) * jnp.exp(seg)
        y = (jnp.einsum('btsh,bshp->bthp', scores, xc)
             + jnp.exp(cum)[..., None] * jnp.einsum('bthn,bhpn->bthp', cc, S))
        last = cum[:, -1]
        S = (jnp.exp(last)[..., None, None] * S
             + jnp.einsum('bsh,bshp,bshn->bhpn', jnp.exp(last[:, None] - cum), xc, bc))
        return S, y

    S0 = jnp.zeros((bsz, H, P, bm.shape[-1]), f32)
    _, ys = lax.scan(step, S0, xs)
    return _from_chunks(ys).astype(dtype)


def _mlstm_scan(q, k, v, li, lf):
    dtype = v.dtype
    f32 = jnp.float32
    bsz, T, H, dk = q.shape
    dv = v.shape[-1]
    xs = tuple(_to_chunks(t.astype(f32)) for t in (q, k, v, li, lf))
    mask = _lower_tri()

    def step(carry, inp):
        Cs, ns, m = carry
        qc, kc, vc, lic, lfc = inp
        b = jnp.cumsum(lfc, axis=1)
        seg = jnp.where(mask[None, :, :, None], b[:, :, None] - b[:, None] + lic[:, None], -jnp.inf)
        inter = m[:, None, :] + b
        m_t = jnp.maximum(inter, jnp.max(seg, axis=2))
        w = jnp.exp(seg - m_t[:, :, None, :])
        w_inter = jnp.exp(inter - m_t)
        qk = jnp.einsum('bthd,bshd->btsh', qc, kc) * w
        num = (jnp.einsum('btsh,bshv->bthv', qk, vc)
               + w_inter[..., None] * jnp.einsum('bhvd,bthd->bthv', Cs, qc))
        den = jnp.sum(qk, axis=2) + w_inter * jnp.einsum('bhd,bthd->bth', ns, qc)
        h = num / jnp.maximum(jnp.abs(den), jnp.exp(-m_t))[..., None]
        m_new = m_t[:, -1]
        ws = jnp.exp(b[:, -1:] - b + lic - m_new[:, None])
        keep = jnp.exp(m + b[:, -1] - m_new)
        Cs = keep[..., None, None] * Cs + jnp.einsum('bsh,bshv,bshd->bhvd', ws, vc, kc)
        ns = keep[..., None] * ns + jnp.einsum('bsh,bshd->bhd', ws, kc)
        return (Cs, ns, m_new), h

    init = (jnp.zeros((bsz, H, dv, dk), f32), jnp.zeros((bsz, H, dk), f32), jnp.zeros((bsz, H), f32))
    _, hs = lax.scan(step, init, xs)
    return _from_chunks(hs).astype(dtype)


def _gla_scan(q, k, v, lg):
    dtype = v.dtype
    f32 = jnp.float32
    bsz, T, H, dk = q.shape
    dv = v.shape[-1]
    xs = tuple(_to_chunks(t.astype(f32)) for t in (q, k, v, lg))
    mask = _lower_tri()

    def step(S, inp):
        qc, kc, vc, gc = inp
        b = jnp.cumsum(gc, axis=1)
        seg = jnp.where(mask[None, :, :, None, None], b[:, :, None] - b[:, None], -jnp.inf)
        att = jnp.einsum('bthk,bshk,btshk->btsh', qc, kc, jnp.exp(seg))
        o = (jnp.einsum('btsh,bshv->bthv', att, vc)
             + jnp.einsum('bthk,bhkv->bthv', qc * jnp.exp(b), S))
        last = b[:, -1]
        S = (jnp.exp(last)[..., None] * S
             + jnp.einsum('bshk,bshv->bhkv', kc * jnp.exp(last[:, None] - b), vc))
        return S, o

    _, os_ = lax.scan(step, jnp.zeros((bsz, H, dk, dv), f32), xs)
    return _from_chunks(os_).astype(dtype)


def _mixer(h, n_ctx, keep_ctx, orders, p):
    rev, col_f, col_f_inv, col_b, col_b_inv = orders
    bsz, T, _ = h.shape
    (s_x, s_z, s_b, s_c, s_dtf, s_dtb,
     m_q, m_k, m_v, m_o, m_if, m_ib, m_ff, m_fb,
     g_q, g_k, g_v, g_g, g_af, g_ab,
     gate_ssd, gate_ml, gate_gla) = jnp.split(h @ p['w_in'], np.cumsum(IN_SPLITS)[:-1], axis=-1)

    xbc = jax.nn.silu(_conv_two(jnp.concatenate([s_x, s_b, s_c], axis=-1), n_ctx, p['ssd_conv_w'], p['ssd_conv_b']))
    s_x, s_b, s_c = jnp.split(xbc, [SSD_WIDTH, SSD_WIDTH + SSD_BC], axis=-1)
    xh = s_x.reshape(bsz, T, SSD_HEADS, SSD_HEAD_DIM)
    rep = SSD_HEADS // SSD_GROUPS
    bm = jnp.repeat(s_b.reshape(bsz, T, SSD_GROUPS, SSD_STATE), rep, axis=2)
    cm = jnp.repeat(s_c.reshape(bsz, T, SSD_GROUPS, SSD_STATE), rep, axis=2)
    dt_f = jax.nn.softplus(s_dtf + p['ssd_dt_bias'][0])
    dt_b = jax.nn.softplus(s_dtb + p['ssd_dt_bias'][1])
    y = (_ssd_scan(xh, dt_f, bm, cm, p['ssd_a_log'][0])
         + _take(_ssd_scan(_take(xh, rev), _take(dt_b, rev), _take(bm, rev), _take(cm, rev), p['ssd_a_log'][1]), rev)
         + p['ssd_d'][:, None] * xh)
    y_ssd = _rmsnorm(y.reshape(bsz, T, SSD_WIDTH) * jax.nn.silu(s_z), p['ssd_norm_w'], SSD_GROUPS)

    qk = jax.nn.silu(_conv_two(jnp.concatenate([m_q, m_k], axis=-1), n_ctx, p['ml_conv_w'], p['ml_conv_b']))
    q = qk[..., :ML_QK_WIDTH].reshape(bsz, T, ML_HEADS, ML_QK_DIM)
    k = qk[..., ML_QK_WIDTH:].reshape(bsz, T, ML_HEADS, ML_QK_DIM) * ML_QK_DIM ** -0.5
    v = m_v.reshape(bsz, T, ML_HEADS, ML_V_DIM)
    li_f = m_if + p['ml_i_bias'][0]
    li_b = m_ib + p['ml_i_bias'][1]
    lf_f = jax.nn.log_sigmoid(m_ff + p['ml_f_bias'][0])
    lf_b = jax.nn.log_sigmoid(m_fb + p['ml_f_bias'][1])
    hm = (_mlstm_scan(q, k, v, li_f, lf_f)
          + _take(_mlstm_scan(_take(q, rev), _take(k, rev), _take(v, rev), _take(li_b, rev), _take(lf_b, rev)), rev))
    y_ml = _rmsnorm(hm.reshape(bsz, T, ML_V_WIDTH), p['ml_norm_w'], ML_HEADS) * jax.nn.sigmoid(m_o)

    gq = g_q.reshape(bsz, T, GLA_HEADS, GLA_K_DIM) * GLA_K_DIM ** -0.5
    gk = g_k.reshape(bsz, T, GLA_HEADS, GLA_K_DIM)
    gv = g_v.reshape(bsz, T, GLA_HEADS, GLA_V_DIM)
    lg_f = (jax.nn.log_sigmoid(g_af @ p['gla_a_up'][0] + p['gla_a_bias'][0]) / GLA_TAU).reshape(bsz, T, GLA_HEADS, GLA_K_DIM)
    lg_b = (jax.nn.log_sigmoid(g_ab @ p['gla_a_up'][1] + p['gla_a_bias'][1]) / GLA_TAU).reshape(bsz, T, GLA_HEADS, GLA_K_DIM)
    o = (_take(_gla_scan(_take(gq, col_f), _take(gk, col_f), _take(gv, col_f), _take(lg_f, col_f)), col_f_inv)
         + _take(_gla_scan(_take(gq, col_b), _take(gk, col_b), _take(gv, col_b), _take(lg_b, col_b)), col_b_inv))
    y_gla = _rmsnorm(o.reshape(bsz, T, GLA_V_WIDTH), p['gla_norm_w'], GLA_HEADS) * jax.nn.silu(g_g)

    st = 0 if keep_ctx else n_ctx
    merged = (jax.nn.sigmoid(gate_ssd[:, st:]) * (y_ssd[:, st:] @ p['w_b_ssd'])
              + jax.nn.sigmoid(gate_ml[:, st:]) * (y_ml[:, st:] @ p['w_b_ml'])
              + jax.nn.sigmoid(gate_gla[:, st:]) * (y_gla[:, st:] @ p['w_b_gla']))
    return merged @ p['w_out']


def _swiglu(h, w_in, w_out):
    g, u = jnp.split(h @ w_in, 2, axis=-1)
    return (jax.nn.silu(g) * u) @ w_out


def setup_inputs(seed: int = 0) -> dict:
    key = jax.random.key(seed)
    ks = jax.random.split(key, 32)
    f32 = jnp.float32
    L = DEPTH

    def nrm(k, shape, scale):
        return jax.random.normal(k, shape, f32) * scale

    def gain(k, shape):
        return 1.0 + 0.02 * jax.random.normal(k, shape, f32)

    dt = jnp.exp(jax.random.uniform(ks[10], (L, 2, SSD_HEADS), f32, math.log(DT_MIN), math.log(DT_MAX)))
    return {
        'x': nrm(ks[0], (BATCH, SEQ, D_MODEL), 1.0),
        'c': nrm(ks[1], (BATCH, D_MODEL), 1.0),
        'ctx': nrm(ks[2], (BATCH, CTX_LEN, D_MODEL), 1.0),
        'c_ctx': nrm(ks[3], (D_MODEL,), 1.0),
        'w_mod': nrm(ks[4], (L, D_MODEL, 6 * D_MODEL), D_MODEL ** -0.5),
        'b_mod': nrm(ks[5], (L, 6 * D_MODEL), 0.02),
        'norm_mix_w': gain(ks[6], (L, D_MODEL)),
        'norm_ffn_w': gain(ks[7], (L, D_MODEL)),
        'w_in': nrm(ks[8], (L, D_MODEL, D_IN), D_MODEL ** -0.5),
        'ssd_conv_w': nrm(ks[9], (L, CONV_K, SSD_XBC), CONV_K ** -0.5),
        'ssd_conv_b': nrm(ks[11], (L, SSD_XBC), 0.02),
        'ssd_dt_bias': dt + jnp.log(-jnp.expm1(-dt)),
        'ssd_a_log': jnp.log(jax.random.uniform(ks[12], (L, 2, SSD_HEADS), f32, 1.0, 16.0)),
        'ssd_d': gain(ks[13], (L, SSD_HEADS)),
        'ssd_norm_w': gain(ks[14], (L, SSD_WIDTH)),
        'ml_conv_w': nrm(ks[15], (L, CONV_K, 2 * ML_QK_WIDTH), CONV_K ** -0.5),
        'ml_conv_b': nrm(ks[16], (L, 2 * ML_QK_WIDTH), 0.02),
        'ml_i_bias': nrm(ks[17], (L, 2, ML_HEADS), 0.1),
        'ml_f_bias': jax.random.uniform(ks[18], (L, 2, ML_HEADS), f32, 3.0, 6.0),
        'ml_norm_w': gain(ks[19], (L, ML_V_WIDTH)),
        'gla_a_up': nrm(ks[20], (L, 2, GLA_RANK, GLA_K_WIDTH), GLA_RANK ** -0.5),
        'gla_a_bias': nrm(ks[21], (L, 2, GLA_K_WIDTH), 0.02),
        'gla_norm_w': gain(ks[22], (L, GLA_V_WIDTH)),
        'w_b_ssd': nrm(ks[23], (L, SSD_WIDTH, D_MODEL), SSD_WIDTH ** -0.5),
        'w_b_ml': nrm(ks[24], (L, ML_V_WIDTH, D_MODEL), ML_V_WIDTH ** -0.5),
        'w_b_gla': nrm(ks[25], (L, GLA_V_WIDTH, D_MODEL), GLA_V_WIDTH ** -0.5),
        'w_out': nrm(ks[26], (L, D_MODEL, D_MODEL), D_MODEL ** -0.5),
        'w_ffn_in': nrm(ks[27], (L, D_MODEL, 2 * D_FF), D_MODEL ** -0.5),
        'w_ffn_out': nrm(ks[28], (L, D_FF, D_MODEL), D_FF ** -0.5),
        'final_norm_w': gain(ks[29], (D_MODEL,)),
    }


def reference(x, c, ctx, c_ctx, w_mod, b_mod, norm_mix_w, norm_ffn_w, w_in,
              ssd_conv_w, ssd_conv_b, ssd_dt_bias, ssd_a_log, ssd_d, ssd_norm_w,
              ml_conv_w, ml_conv_b, ml_i_bias, ml_f_bias, ml_norm_w,
              gla_a_up, gla_a_bias, gla_norm_w,
              w_b_ssd, w_b_ml, w_b_gla, w_out, w_ffn_in, w_ffn_out, final_norm_w):
    n_ctx, n_lat = ctx.shape[1], x.shape[1]
    rows = n_lat // GRID_W
    ctx_ids = np.arange(n_ctx)
    lat_ids = n_ctx + np.arange(n_lat)
    col_ids = n_ctx + np.arange(n_lat).reshape(rows, GRID_W).T.reshape(-1)
    rev = np.concatenate([ctx_ids[::-1], lat_ids[::-1]]).astype(np.int32)
    col_f = np.concatenate([ctx_ids, col_ids]).astype(np.int32)
    col_b = np.concatenate([ctx_ids[::-1], col_ids[::-1]]).astype(np.int32)
    orders = (rev, col_f, np.argsort(col_f).astype(np.int32), col_b, np.argsort(col_b).astype(np.int32))

    silu_c = jax.nn.silu(c)
    silu_cc = jax.nn.silu(c_ctx)
    x_ctx, x_lat = ctx, x
    for l in range(DEPTH):
        last = l == DEPTH - 1
        sh1, sc1, g1, sh2, sc2, g2 = jnp.split((silu_c @ w_mod[l] + b_mod[l])[:, None, :], 6, axis=-1)
        csh1, csc1, cg1, csh2, csc2, cg2 = jnp.split(silu_cc @ w_mod[l] + b_mod[l], 6)
        h = jnp.concatenate([_rmsnorm(x_ctx, norm_mix_w[l]) * (1.0 + csc1) + csh1,
                             _rmsnorm(x_lat, norm_mix_w[l]) * (1.0 + sc1) + sh1], axis=1)
        p = {
            'w_in': w_in[l], 'ssd_conv_w': ssd_conv_w[l], 'ssd_conv_b': ssd_conv_b[l],
            'ssd_dt_bias': ssd_dt_bias[l], 'ssd_a_log': ssd_a_log[l], 'ssd_d': ssd_d[l],
            'ssd_norm_w': ssd_norm_w[l], 'ml_conv_w': ml_conv_w[l], 'ml_conv_b': ml_conv_b[l],
            'ml_i_bias': ml_i_bias[l], 'ml_f_bias': ml_f_bias[l], 'ml_norm_w': ml_norm_w[l],
            'gla_a_up': gla_a_up[l], 'gla_a_bias': gla_a_bias[l], 'gla_norm_w': gla_norm_w[l],
            'w_b_ssd': w_b_ssd[l], 'w_b_ml': w_b_ml[l], 'w_b_gla': w_b_gla[l], 'w_out': w_out[l],
        }
        out = _mixer(h, n_ctx, not last, orders, p)
        x_lat = x_lat + g1 * out[:, -n_lat:]
        h_lat = _rmsnorm(x_lat, norm_ffn_w[l]) * (1.0 + sc2) + sh2
        x_lat = x_lat + g2 * _swiglu(h_lat, w_ffn_in[l], w_ffn_out[l])
        if not last:
            x_ctx = x_ctx + cg1 * out[:, :n_ctx]
            h_ctx = _rmsnorm(x_ctx, norm_ffn_w[l]) * (1.0 + csc2) + csh2
            x_ctx = x_ctx + cg2 * _swiglu(h_ctx, w_ffn_in[l], w_ffn_out[l])
    return _rmsnorm(x_lat, final_norm_w)
```

```python
import contextlib
import math
import os
import numpy as np
import concourse.bass as bass
import concourse.mybir as mybir
from concourse.bass_utils import run_bass_kernel_spmd

F32 = mybir.dt.float32
BF16 = mybir.dt.bfloat16
AF = mybir.ActivationFunctionType
ALU = mybir.AluOpType

NL = 2
T = 2304
NT = 18
NCTX = 2
D = 1024
DIN = 11600
DFF = 2816
EPS = 1e-6
LN_ISQ = math.log(128.0 ** -0.5)
C_ID, C_U, C_L, C_LS, C_US, C_ONE = range(6)


class Buf:
    def __init__(self, name, t=None):
        self.name = name
        self.t = t
        self.last_w = None
        self.readers = {}
        self.dsem = None

    def __getitem__(self, idx):
        return self.t[idx]


import os as _os
POOL_TO = _os.environ.get("POOL_TO")
SEM_ROT = int(_os.environ.get("SEM_ROT", "6000"))
FINE_IL = _os.environ.get("FINE_IL", "1") == "1"
FINE_MASK = int(_os.environ.get("FINE_MASK", "255"))


class Prog:
    ENGS = ("pe", "dve", "act", "pool", "sp")

    def __init__(self, nc):
        self.nc = nc
        self.streams = {e: [] for e in self.ENGS}
        self.count = {}
        self.known = {e: {} for e in self.ENGS}
        self.semkeys = []
        self.free_dsems = []
        self.engkey = {}
        for e in ("pe", "dve", "act", "pool"):
            self.engkey[e] = self._newsem("eng_" + e)
        self.ninst = 0

    def _newsem(self, key):
        self.semkeys.append(key)
        self.count[key] = 0
        return key

    def buf(self, name, t=None):
        return Buf(name, t)

    def _dsem(self, b):
        if b.dsem is None:
            if self.free_dsems:
                b.dsem = self.free_dsems.pop()
            else:
                b.dsem = self._newsem("dma%d" % len(self.semkeys))
        return b.dsem

    def release(self, bufs):
        for b in bufs:
            if b.dsem is not None:
                self.free_dsems.append(b.dsem)
                b.dsem = None

    def _need(self, eng, ev, waits):
        if ev is None:
            return
        k, v = ev
        if self.known[eng].get(k, 0) >= v:
            return
        if waits.get(k, 0) < v:
            waits[k] = v

    def _deps(self, eng, reads, writes, skip_sem=None, same_eng_sync=True):
        waits = {}
        own = self.engkey.get(eng)
        for r in reads:
            self._need(eng, r.last_w, waits)
            if getattr(r, "psum", False):
                for k, v in r.readers.items():
                    if k != own:
                        self._need(eng, (k, v), waits)
        for w in writes:
            if w.last_w is not None and w.last_w[0] != skip_sem:
                self._need(eng, w.last_w, waits)
            for k, v in w.readers.items():
                if k != skip_sem:
                    self._need(eng, (k, v), waits)
        if not same_eng_sync:
            waits.pop(own, None)
        for k, v in waits.items():
            self.streams[eng].append(("wait", k, v))
            self.known[eng][k] = v

    def op(self, eng, fn, reads=(), writes=(), same_eng_sync=True):
        if eng == "pool" and POOL_TO is not None:
            eng = POOL_TO
        self._deps(eng, reads, writes, same_eng_sync=same_eng_sync)
        key = self.engkey[eng]
        self.count[key] += 1
        v = self.count[key]
        self.streams[eng].append(("ins", fn, key, 1))
        self.ninst += 1
        for r in reads:
            if r.readers.get(key, 0) < v:
                r.readers[key] = v
        for w in writes:
            w.last_w = (key, v)
            w.readers = {}
        return v

    def dma(self, q, out_ap, in_ap, src, dst, owner=None, **kw):
        if owner is None:
            owner = dst if (dst is not None and dst.t is not None) else src
        key = self._dsem(owner)
        reads = [src] if src is not None else []
        writes = [dst] if dst is not None else []
        self._deps(q, reads, writes, skip_sem=key)
        self.count[key] += 16
        v = self.count[key]

        def fn(e, out_ap=out_ap, in_ap=in_ap, kw=kw):
            return e.dma_start(out=out_ap, in_=in_ap, **kw)

        self.streams[q].append(("ins", fn, key, 16))
        self.ninst += 1
        if src is not None:
            if src.readers.get(key, 0) < v:
                src.readers[key] = v
        if dst is not None:
            dst.last_w = (key, v)
            dst.readers = {}
        return v

    def barrier(self):
        for e in self.ENGS:
            for k in self.semkeys:
                v = self.count[k]
                if v > self.known[e].get(k, 0):
                    self.streams[e].append(("wait", k, v))
                    self.known[e][k] = v
        for e in list(self.engkey):
            if self.count[self.engkey[e]] > SEM_ROT:
                self.engkey[e] = self._newsem("eng_%s_%d" % (e, len(self.semkeys)))

    def emit(self):
        nc = self.nc
        with contextlib.ExitStack() as st:
            sems = {}
            for k in self.semkeys:
                sems[k] = st.enter_context(nc.semaphore(k))
            block = st.enter_context(nc.Block())

            def replay(e, name):
                for it in self.streams[name]:
                    if it[0] == "wait":
                        e.wait_ge(sems[it[1]], it[2])
                    else:
                        _, fn, key, inc = it
                        fn(e).then_inc(sems[key], inc)

            @block.sync
            def _(e):
                replay(e, "sp")

            @block.tensor
            def _(e):
                replay(e, "pe")

            @block.vector
            def _(e):
                replay(e, "dve")

            @block.scalar
            def _(e):
                replay(e, "act")

            @block.gpsimd
            def _(e):
                replay(e, "pool")


class Rot:
    def __init__(self, bufs):
        self.bufs = bufs
        self.i = 0

    def next(self):
        b = self.bufs[self.i % len(self.bufs)]
        self.i += 1
        return b


class Builder:
    def __init__(self, debug=(), stop_after=None, layers=NL, layer_ids=None):
        self.layer_ids = layer_ids
        self.debug = set(debug)
        self.stop_after = stop_after
        self.layers = layers
        self.nc = bass.Bass("TRN2", target_bir_lowering=False)
        self.P = Prog(self.nc)
        self.uid = 0
        self.rr = 0
        self.nch = int(os.environ.get("SSD_NCH", "99"))
        self.nch1 = int(os.environ.get("SSD_NCH1", "99"))
        self.sub0 = int(os.environ.get("SSD_SUB", "99"))
        self.sub1 = int(os.environ.get("SSD_SUB1", "99"))
        self.sub = 99
        self.stop_occ = int(os.environ.get("STOP_OCC", "1"))

    def din(self, name, shape, dt=F32):
        return self.nc.dram_tensor(name, list(shape), dt, kind="ExternalInput").ap()

    def scr(self, name, shape, dt):
        kind = "ExternalOutput" if name in self.debug else "Internal"
        ap = self.nc.dram_tensor(name, list(shape), dt, kind=kind).ap()
        b = self.P.buf(name)
        b.ap = ap
        return b

    class Phase:
        def __init__(self, bld):
            self.b = bld
            self.st = contextlib.ExitStack()
            self.bufs = []

        def sb(self, name, shape, dt=F32):
            self.b.uid += 1
            t = self.st.enter_context(self.b.nc.sbuf_tensor("%s_%d" % (name, self.b.uid), list(shape), dt))
            bf = self.b.P.buf(name, t)
            self.bufs.append(bf)
            return bf

        def ps(self, name, shape, dt=F32):
            self.b.uid += 1
            t = self.st.enter_context(self.b.nc.psum_tensor("%s_%d" % (name, self.b.uid), list(shape), dt))
            bf = self.b.P.buf(name, t)
            bf.psum = True
            self.bufs.append(bf)
            return bf

        def rot(self, kind, name, shape, dt, n):
            f = self.sb if kind == "sb" else self.ps
            return Rot([f("%s%d" % (name, i), shape, dt) for i in range(n)])

        def close(self):
            self.b.P.barrier()
            self.b.P.release(self.bufs)
            self.st.close()

    def phase(self):
        return Builder.Phase(self)

    def mm(self, out, lhsT, rhs, start, stop, reads, writes):
        self.P.op("pe", lambda e: e.matmul(out, lhsT=lhsT, rhs=rhs, start=start, stop=stop),
                  reads, writes, same_eng_sync=False)

    def act(self, out, in_, func, reads, writes, **kw):
        self.P.op("act", lambda e: e.activation(out=out, in_=in_, func=func, **kw), reads, writes)

    def tt(self, eng, out, in0, in1, op, reads, writes):
        self.P.op(eng, lambda e: e.tensor_tensor(out=out, in0=in0, in1=in1, op=op), reads, writes)

    def ts(self, eng, out, in0, s1, s2, op0, op1, reads, writes):
        if s2 is None:
            self.P.op(eng, lambda e: e.tensor_scalar(out=out, in0=in0, scalar1=s1, scalar2=None, op0=op0), reads, writes)
        else:
            self.P.op(eng, lambda e: e.tensor_scalar(out=out, in0=in0, scalar1=s1, scalar2=s2, op0=op0, op1=op1), reads, writes)

    def stt(self, eng, out, in0, scalar, in1, op0, op1, reads, writes):
        eng = "dve"
        self.P.op(eng, lambda e: e.scalar_tensor_tensor(out=out, in0=in0, scalar=scalar, in1=in1, op0=op0, op1=op1),
                  reads, writes)

    def cp(self, eng, out, in_, reads, writes):
        if eng == "act":
            if os.environ.get("ACTCOPY", "act") == "act":
                self.P.op("act", lambda e: e.activation(out=out, in_=in_, func=AF.Copy), reads, writes)
            else:
                self.P.op("act", lambda e: e.copy(out=out, in_=in_), reads, writes)
        else:
            self.P.op(eng, lambda e: e.tensor_copy(out=out, in_=in_), reads, writes)

    def memset(self, eng, ap, val, writes):
        self.P.op(eng, lambda e: e.memset(ap, val), [], writes)

    def ld(self, out_ap, in_ap, src, dst, q="sp"):
        self.P.dma(q, out_ap, in_ap, src, dst)

    def stq(self, out_ap, in_ap, src, dst, q="sp"):
        self.P.dma(q, out_ap, in_ap, src, dst)

    def interleave(self, gens):
        gens = list(gens)
        while gens:
            for g in list(gens):
                try:
                    next(g)
                except StopIteration:
                    gens.remove(g)

    def evac_eng(self):
        self.rr += 1
        return "act" if self.rr % 2 else "dve"

    def build(self):
        nc, P = self.nc, self.P
        I = {}
        I["x"] = self.din("x", [2048, D])
        I["ctx"] = self.din("ctx", [256, D])
        I["cvec"] = self.din("cvec", [128, 8, 2])
        I["w_mod"] = self.din("w_mod", [NL, D, 6 * D])
        I["b_modT"] = self.din("b_modT", [NL, 128, 48])
        I["b_mod"] = self.din("b_mod", [NL, 6 * D])
        I["nmwT"] = self.din("nmwT", [NL, 128, 8])
        I["nfwT"] = self.din("nfwT", [NL, 128, 8])
        I["fnw"] = self.din("fnw", [1, D])
        I["w_in"] = self.din("w_in", [NL, D, DIN])
        I["sconvT"] = self.din("sconvT", [NL, 128, 10, 5])
        I["sconvb"] = self.din("sconvb", [NL, 128, 10])
        I["mconvT"] = self.din("mconvT", [NL, 128, 8, 5])
        I["mconvb"] = self.din("mconvb", [NL, 128, 8])
        I["sdtb"] = self.din("sdtb", [NL, 32])
        I["salog"] = self.din("salog", [NL, 32])
        I["sd"] = self.din("sd", [NL, 16])
        I["snwT"] = self.din("snwT", [NL, 128, 8])
        I["mib"] = self.din("mib", [NL, 8])
        I["mfb"] = self.din("mfb", [NL, 8])
        I["mnwT"] = self.din("mnwT", [NL, 128, 8])
        I["aup"] = self.din("aup", [NL, 2, 17, 512])
        I["gnwT"] = self.din("gnwT", [NL, 128, 8])
        for n in ("w_b_ssd", "w_b_ml", "w_b_gla", "w_out"):
            I[n] = self.din(n, [NL, D, D])
        I["w_ffn_in"] = self.din("w_ffn_in", [NL, D, 2 * DFF])
        I["w_ffn_out"] = self.din("w_ffn_out", [NL, DFF, D])
        I["consts"] = self.din("consts", [128, 6, 128])
        I["sel"] = self.din("sel", [2, 2, 128])
        self.I = I
        self.out_ap = nc.dram_tensor("out", [2048, D], F32, kind="ExternalOutput").ap()
        self.outb = P.buf("out")

        S = {}
        S["xs"] = self.scr("xs", [T, D], F32)
        S["sbc_fm"] = self.scr("sbc_fm", [256, T], BF16)
        S["sx_tm"] = self.scr("sx_tm", [T, D], BF16)
        S["sb_tm"] = self.scr("sb_tm", [T, 128], BF16)
        S["sz_tm"] = self.scr("sz_tm", [T, D], BF16)
        S["sdt"] = self.scr("sdt", [T, 32], F32)
        S["mq_fm"] = self.scr("mq_fm", [512, T], BF16)
        S["mk_fm"] = self.scr("mk_fm", [512, T], BF16)
        S["mk_tm"] = self.scr("mk_tm", [T, 512], BF16)
        S["mv_tm"] = self.scr("mv_tm", [T, D], BF16)
        S["mo_tm"] = self.scr("mo_tm", [T, D], BF16)
        S["mg"] = self.scr("mg", [T, 16], F32)
        S["gq_fm"] = self.scr("gq_fm", [512, T], BF16)
        S["gk_fm"] = self.scr("gk_fm", [512, T], BF16)
        S["gk_tm"] = self.scr("gk_tm", [T, 512], BF16)
        S["gv_tm"] = self.scr("gv_tm", [T, D], BF16)
        S["gg_tm"] = self.scr("gg_tm", [T, D], BF16)
        S["ga_fm"] = self.scr("ga_fm", [32, T], BF16)
        S["gates"] = self.scr("gates", [T, 3 * D], BF16)
        for n in ("ys_f", "ys_b", "ym_f", "ym_b", "yg_f", "yg_b"):
            S[n] = self.scr(n, [T, D], BF16)
        S["aT"] = self.scr("aT", [DFF, T], BF16)
        self.S = S

        glob = self.phase()
        self.glob = glob
        cst = glob.sb("cst", [128, 6, 128], F32)
        cstb = glob.sb("cstb", [128, 6, 128], BF16)
        sel = glob.sb("sel", [2, 2, 128], F32)
        scT = glob.sb("scT", [128, 8, 2], F32)
        self.cst, self.cstb, self.sel, self.scT = cst, cstb, sel, scT
        self.ld(cst[:], I["consts"], None, cst)
        self.ld(sel[:], I["sel"], None, sel)
        self.ld(scT[:], I["cvec"], None, scT)
        self.cp("dve", cstb[:], cst[:], [cst], [cstb])
        self.act(scT[:], scT[:], AF.Silu, [scT], [scT])
        dummy = P.buf("dummy")
        P.dma("sp", S["xs"].ap[0:256, :], I["ctx"], None, S["xs"], owner=dummy)
        P.dma("sp", S["xs"].ap[256:T, :], I["x"], None, S["xs"], owner=dummy)
        P.barrier()

        self.modT = glob.sb("modT", [128, 48, 2], F32)
        self.A1 = glob.sb("A1", [128, 8, 2], F32)
        self.A2 = glob.sb("A2", [128, 8, 2], F32)
        self.G = [[glob.sb("G%d%d" % (a, w), [128, D], F32) for w in range(2)] for a in range(2)]

        stages = ["mod", "n1", "proj", "ssd", "ml", "gla", "epi", "ffn"]
        ndummy = int(os.environ.get("NDUMMY", "0"))
        if ndummy:
            dph = self.phase()
            dps = dph.ps("dps", [128, 128])
            for _ in range(ndummy):
                self.mm(dps[:], cstb[:, C_ID, :], cstb[:, C_ID, :], True, True, [cstb], [dps])
            dph.close()
        done = False
        for l in (self.layer_ids if self.layer_ids is not None else range(self.layers)):
            for stg in stages:
                if l == 0 and stg in os.environ.get("SKIP0", "").split(","):
                    continue
                getattr(self, "ph_" + stg)(l)
                if stg == "ssd":
                    for _ in range(int(os.environ.get("REPEAT_SSD", "1")) - 1):
                        self.ph_ssd(l)
                if self.stop_after == (l, stg):
                    self.stop_occ -= 1
                    if self.stop_occ <= 0:
                        done = True
                        break
            if done:
                break
        P.barrier()
        glob.st.close()
        P.emit()
        return nc

    def wview(self, w_ap, c0, n):
        return w_ap.rearrange("(k p) n -> p k n", p=128)[:, :, c0:c0 + n]

    def ph_mod(self, l):
        I, P = self.I, self.P
        ph = self.phase()
        wst = ph.rot("sb", "wst", [128, 8, 512], F32, 3)
        pmf = ph.ps("pm", [128, 512])
        pm = pmf
        pmv = pmf.t[:, 0:96].rearrange("p (j w) -> p j w", w=2)
        prow = ph.rot("ps", "prow", [2, 512], F32, 2)
        pbc = ph.rot("ps", "pbc", [128, 512], F32, 2)
        brow = ph.sb("brow", [2, 6 * D], F32)
        rows = ph.sb("rows", [2, 6 * D], F32)
        nmw = ph.sb("nmw", [128, 8], F32)
        nfw = ph.sb("nfw", [128, 8], F32)
        self.ld(nmw[:], I["nmwT"][l], None, nmw)
        self.ld(nfw[:], I["nfwT"][l], None, nfw)
        self.ld(brow[:], I["b_mod"][l].partition_broadcast(2), None, brow)
        scT = self.scT
        cst = self.cst
        wq = {}

        def ldw(g):
            w_ = wst.next()
            self.ld(w_[:], self.wview(I["w_mod"][l], g * 512, 512), None, w_, q=("sp" if g % 2 == 0 else "act"))
            wq[g] = w_

        ldw(0)
        ldw(1)
        for g in range(12):
            if g + 2 < 12:
                ldw(g + 2)
            w = wq.pop(g)
            pr = prow.next()
            for k in range(8):
                self.mm(pr[:, :], scT[:, k, :], w[:, k, :], k == 0, k == 7, [w, scT], [pr])
            self.tt("dve", rows[:, g * 512:(g + 1) * 512], pr[:, :], brow[:, g * 512:(g + 1) * 512], ALU.add, [pr, brow], [rows])
        for j in range(48):
            self.mm(pmv[:, j, :], rows[:, j * 128:(j + 1) * 128], cst[0:2, C_ID, 0:2], True, True, [rows, cst], [pm])
        self.cp("dve", self.modT[:], pmv, [pm], [self.modT])
        modT = self.modT
        for wv in range(2):
            self.stt("dve", self.A1[:, :, wv], modT[:, 8:16, wv], 1.0, nmw[:], ALU.add, ALU.mult, [modT, nmw], [self.A1])
            self.stt("dve", self.A2[:, :, wv], modT[:, 32:40, wv], 1.0, nfw[:], ALU.add, ALU.mult, [modT, nfw], [self.A2])
        for a_, c0 in ((0, 2 * D), (1, 5 * D)):
            for wv in range(2):
                for half in range(2):
                    pb = pbc.next()
                    self.mm(pb[:, :], self.sel[:, wv, :], rows[:, c0 + half * 512:c0 + (half + 1) * 512], True, True,
                            [self.sel, rows], [pb])
                    self.cp(self.evac_eng(), self.G[a_][wv][:, half * 512:(half + 1) * 512], pb[:, :], [pb], [self.G[a_][wv]])
        ph.close()

    def norm_to_hT(self, ph, hT, A, Bcol0, tiles, tok0):
        S, P = self.S, self.P
        xr = ph.rot("sb", "xr", [128, D], F32, 3)
        junk = ph.sb("junk", [128, D], F32)
        ssr = ph.rot("sb", "ss", [128, 2], F32, 3)
        dgr = ph.rot("sb", "dg", [128, 128], F32, 2)
        ptr = ph.rot("ps", "pT", [128, 4, 128], F32, 3)
        cst = self.cst
        for i in tiles:
            wv = 1 if i < NCTX else 0
            xt = xr.next()
            self.ld(xt[:], S["xs"].ap[i * 128:(i + 1) * 128, :], S["xs"], xt)
            ss = ssr.next()
            self.memset("pool", ss[:], 0.0, [ss])
            self.act(junk[:], xt[:], AF.Square, [xt, ss], [junk, ss], accum_out=ss[:, 0:1])
            self.act(ss[:, 1:2], ss[:, 0:1], AF.Ln, [ss], [ss], scale=1.0 / D, bias=EPS)
            self.act(ss[:, 1:2], ss[:, 1:2], AF.Exp, [ss], [ss], scale=-0.5)
            dg = dgr.next()
            self.ts("dve", dg[:], cst[:, C_ID, :], ss[:, 1:2], None, ALU.mult, None, [cst, ss], [dg])
            for kg in range(2):
                pt = ptr.next()
                for kk in range(4):
                    k = kg * 4 + kk
                    self.mm(pt[:, kk, :], xt[:, k * 128:(k + 1) * 128], dg[:], True, True, [xt, dg], [pt])
                eng = self.evac_eng()
                for kk in range(4):
                    k = kg * 4 + kk
                    o = hT[:, k, i * 128 - tok0:(i + 1) * 128 - tok0]
                    if eng == "act":
                        self.act(o, pt[:, kk, :], AF.Identity, [pt, A, self.modT], [hT],
                                 scale=A[:, k, wv:wv + 1], bias=self.modT[:, Bcol0 + k, wv:wv + 1])
                    else:
                        self.ts("dve", o, pt[:, kk, :], A[:, k, wv:wv + 1], self.modT[:, Bcol0 + k, wv:wv + 1],
                                ALU.mult, ALU.add, [pt, A, self.modT], [hT])

    def ph_n1(self, l):
        self.hph = self.phase()
        self.hT = self.hph.sb("hT", [128, 8, T], BF16)
        ph = self.phase()
        self.norm_to_hT(ph, self.hT, self.A1, 0, range(NT), 0)
        ph.close()

    def load_w_bf(self, stg_rot, wb_rot, w_ap, c0, n, kchunks=8):
        st = stg_rot.next()
        wb = wb_rot.next()
        self.ld(st[:, 0:kchunks, 0:n], self.wview(w_ap, c0, n), None, st)
        self.cp("pool", wb[:, 0:kchunks, 0:n], st[:, 0:kchunks, 0:n], [st], [wb])
        return wb

    def tok_groups(self):
        return [(0, 512), (512, 512), (1024, 512), (1536, 512), (2048, 256)]

    def ph_proj(self, l):
        I, S, P = self.I, self.S, self.P
        hT = self.hT
        ph = self.phase()
        stg = ph.rot("sb", "wstg", [128, 8, 512], F32, 2)
        wbr = ph.rot("sb", "wbf", [128, 8, 512], BF16, 2)
        psr = ph.rot("ps", "pp", [128, 512], F32, 4)
        ptr = ph.rot("ps", "pt", [128, 4, 128], F32, 2)
        w_in = I["w_in"][l]
        cstb = self.cstb
        cst = self.cst

        scw = ph.sb("scw", [128, 10, 5], F32)
        scb = ph.sb("scb", [128, 10], F32)
        mcw = ph.sb("mcw", [128, 8, 5], F32)
        mcb = ph.sb("mcb", [128, 8], F32)
        self.ld(scw[:], I["sconvT"][l], None, scw)
        self.ld(scb[:], I["sconvb"][l], None, scb)
        self.ld(mcw[:], I["mconvT"][l], None, mcw)
        self.ld(mcb[:], I["mconvb"][l], None, mcb)
        upr = ph.rot("sb", "up", [128, T + 8], BF16, 2)
        for ub in upr.bufs:
            self.memset("dve", ub[:], 0.0, [ub])
        dgwr = ph.rot("sb", "dgw", [128, 5, 128], BF16, 2)
        sar = ph.rot("sb", "sact", [128, T], BF16, 2)
        tmo = ph.rot("sb", "tmo", [128, NT, 512], BF16, 1)
        cgroups = [(0, 256), (256, 512), (768, 512), (1280, 512), (1792, 512)]

        def pos(t):
            return t + 2 if t < 256 else t + 6

        conv_groups = [
            (0, 512, scw, scb, 0, None, (S["sx_tm"], 0)),
            (512, 512, scw, scb, 4, None, (S["sx_tm"], 512)),
            (2048, 256, scw, scb, 8, (S["sbc_fm"], 0), (S["sb_tm"], 0)),
            (2336, 512, mcw, mcb, 0, (S["mq_fm"], 0), None),
            (2848, 512, mcw, mcb, 4, (S["mk_fm"], 0), (S["mk_tm"], 0)),
        ]
        tasks = []

        def conv_task(wb, c0, ncols, cw, cb, ct0, fm, tm):
            tmb = tmo.next() if tm is not None else None
            ntile = ncols // 128
            for jj in range(ntile):
                ct = ct0 + jj
                up = upr.next()
                dgw = dgwr.next()
                for j in range(5):
                    self.ts("dve", dgw[:, j, :], cst[:, C_ID, :], cw[:, ct, j:j + 1], None, ALU.mult, None, [cst, cw], [dgw])
                for (t0, tn) in cgroups:
                    pp = psr.next()
                    for k in range(8):
                        self.mm(pp[:, 0:tn], wb[:, k, jj * 128:(jj + 1) * 128], hT[:, k, t0:t0 + tn], k == 0, k == 7, [wb, hT], [pp])
                    self.cp(self.evac_eng(), up[:, pos(t0):pos(t0) + tn], pp[:, 0:tn], [pp], [up])
                sa = sar.next()
                for (t0, tn) in cgroups:
                    pp = psr.next()
                    for j in range(5):
                        p0 = pos(t0) + j - 2
                        self.mm(pp[:, 0:tn], dgw[:, j, :], up[:, p0:p0 + tn], j == 0, j == 4, [dgw, up], [pp])
                    self.act(sa[:, t0:t0 + tn], pp[:, 0:tn], AF.Silu, [pp, cb], [sa], bias=cb[:, ct:ct + 1])
                is_c_tile = (c0 == 2048 and jj == 1)
                if fm is not None:
                    fb, r0 = fm
                    self.stq(fb.ap[r0 + jj * 128:r0 + (jj + 1) * 128, :], sa[:], sa, fb)
                if tm is not None and not is_c_tile:
                    for ig in range(0, NT, 4):
                        n_i = min(4, NT - ig)
                        pt = ptr.next()
                        for ii in range(n_i):
                            i = ig + ii
                            self.mm(pt[:, ii, :], sa[:, i * 128:(i + 1) * 128], cstb[:, C_ID, :], True, True, [sa, cstb], [pt])
                        self.cp(self.evac_eng(), tmb[:, ig:ig + n_i, jj * 128:(jj + 1) * 128], pt[:, 0:n_i, :], [pt], [tmb])
            if tm is not None:
                tb, tc0 = tm
                ncs = 128 if c0 == 2048 else ncols
                self.stq(tb.ap.rearrange("(i p) c -> p i c", p=128)[:, :, tc0:tc0 + ncs], tmb[:, :, 0:ncs], tmb, tb)


        for cg in conv_groups:
            tasks.append((cg[0], cg[1], (lambda wb, cg=cg: conv_task(wb, *cg))))

        gfm = ph.rot("sb", "gfm", [128, T], BF16, 2)

        def gla_task(wb, dst):
            for jj in range(4):
                gb = gfm.next()
                self.gla_fm_tile(psr, wb[:, :, jj * 128:(jj + 1) * 128], 128, hT, gb, wb)
                self.stq(dst.ap[jj * 128:(jj + 1) * 128, :], gb[:], gb, dst)

        def ga_task(wb):
            gb = gfm.next()
            self.gla_fm_tile(psr, wb[:, :, 0:32], 32, hT, gb, wb)
            self.stq(S["ga_fm"].ap[:, :], gb[0:32, :], gb, S["ga_fm"])

        for (c0, dst) in ((5424, S["gq_fm"]), (5936, S["gk_fm"])):
            tasks.append((c0, 512, (lambda wb, dst=dst: gla_task(wb, dst))))
        tasks.append((8496, 32, ga_task))

        tmg = []
        for h in range(2):
            tmg.append((1024 + h * 512, 512, S["sz_tm"], h * 512, AF.Silu, BF16))
        for h in range(2):
            tmg.append((7472 + h * 512, 512, S["gg_tm"], h * 512, AF.Silu, BF16))
        for h in range(2):
            tmg.append((4384 + h * 512, 512, S["mo_tm"], h * 512, AF.Sigmoid, BF16))
        for h in range(6):
            tmg.append((8528 + h * 512, 512, S["gates"], h * 512, AF.Sigmoid, BF16))
        for h in range(2):
            tmg.append((3360 + h * 512, 512, S["mv_tm"], h * 512, AF.Copy, BF16))
        tmg.append((5936, 512, S["gk_tm"], 0, AF.Copy, BF16))
        for h in range(2):
            tmg.append((6448 + h * 512, 512, S["gv_tm"], h * 512, AF.Copy, BF16))
        tmg.append((2304, 32, S["sdt"], 0, AF.Copy, F32))
        tmg.append((5408, 16, S["mg"], 0, AF.Copy, F32))
        stb = ph.rot("sb", "stb", [128, 512], BF16, 4)
        stf = ph.rot("sb", "stf", [128, 32], F32, 3)
        def tm_task(wb, c0, ncols, dst, dc0, func, dt):
            for i in range(NT):
                pp = psr.next()
                for k in range(8):
                    self.mm(pp[:, 0:ncols], hT[:, k, i * 128:(i + 1) * 128], wb[:, k, 0:ncols], k == 0, k == 7, [hT, wb], [pp])
                so = stb.next() if dt == BF16 else stf.next()
                if func == AF.Copy:
                    self.cp(self.evac_eng(), so[:, 0:ncols], pp[:, 0:ncols], [pp], [so])
                else:
                    self.act(so[:, 0:ncols], pp[:, 0:ncols], func, [pp], [so])
                self.stq(dst.ap[i * 128:(i + 1) * 128, dc0:dc0 + ncols], so[:, 0:ncols], so, dst)

        for tg in tmg:
            tasks.append((tg[0], tg[1], (lambda wb, tg=tg: tm_task(wb, *tg))))
        nxt = self.load_w_bf(stg, wbr, w_in, tasks[0][0], tasks[0][1])
        for ti, (c0, ncols, fn) in enumerate(tasks):
            wb = nxt
            if ti + 1 < len(tasks):
                nxt = self.load_w_bf(stg, wbr, w_in, tasks[ti + 1][0], tasks[ti + 1][1])
            fn(wb)
        ph.close()
        self.hph.close()

    def gla_fm_tile(self, psr, lhsT, m, hT, gb, wbuf):
        pp = psr.next()
        for k in range(8):
            self.mm(pp[0:m, 0:256], lhsT[:, k, :], hT[:, k, 0:256], k == 0, k == 7, [hT, wbuf], [pp])
        self.cp(self.evac_eng(), gb[0:m, 0:256], pp[0:m, 0:256], [pp], [gb])
        for q in range(4):
            pp = psr.next()
            t0 = 256 + q * 512
            for k in range(8):
                self.mm(pp[0:m, :], lhsT[:, k, :], hT[:, k, t0:t0 + 512], k == 0, k == 7, [hT, wbuf], [pp])
            dst = gb[0:m, 256:T].rearrange("p (c r) -> p r c", r=32)[:, q * 8:(q + 1) * 8, :]
            self.cp(self.evac_eng(), dst, pp[0:m, :].rearrange("p (r c) -> p r c", c=64), [pp], [gb])

    def chunk_order(self, d):
        if d == 0:
            return list(range(NT))
        return [1, 0] + list(range(NT - 1, 1, -1))

    def row_bc(self, ph, name, dram_row_ap, n):
        t = ph.sb(name, [128, n], F32)
        self.ld(t[:], dram_row_ap.partition_broadcast(128), None, t)
        return t

    def ph_ssd(self, l):
        I, S, P = self.I, self.S, self.P
        self.sub = self.sub0 if getattr(self, "ssd_calls", 0) == 0 else self.sub1
        ph = self.phase()
        cst, cstb = self.cst, self.cstb
        BT = ph.sb("BT", [64, 2, T], BF16)
        CT = ph.sb("CT", [64, 2, T], BF16)
        self.ld(BT[:], S["sbc_fm"].ap[0:128, :].rearrange("(g n) t -> n g t", n=64), S["sbc_fm"], BT)
        self.ld(CT[:], S["sbc_fm"].ap[128:256, :].rearrange("(g n) t -> n g t", n=64), S["sbc_fm"], CT)
        dtb = self.row_bc(ph, "dtb", I["sdtb"][l], 32)
        negA = self.row_bc(ph, "negA", I["salog"][l], 32)
        self.act(negA[:], negA[:], AF.Exp, [negA], [negA])
        self.ts("dve", negA[:], negA[:], -1.0, None, ALU.mult, None, [negA], [negA])
        Sts = [ph.sb("St%d" % d_, [64, 16, 64], F32) for d_ in range(2)]
        Sbs = [ph.sb("Sb%d" % d_, [64, 16, 64], BF16) for d_ in range(2)]
        smr = ph.rot("sb", "sm", [128, 6, 16], F32, 4)
        rhsA = ph.rot("sb", "rhsA", [128, 16, 128], F32, 3)
        Dm = ph.rot("sb", "Dm", [128, 16, 128], BF16, 4)
        CBm = ph.rot("sb", "CBm", [128, 2, 128], BF16, 4)
        sTr = ph.rot("sb", "sT", [128, 16, 128], BF16, 4)
        xdtr = ph.rot("sb", "xdt", [128, 16, 64], BF16, 4)
        xhr = ph.rot("sb", "xh", [128, 16, 64], BF16, 4)
        yintr = ph.rot("sb", "yint", [128, 16, 64], F32, 4)
        yor = ph.rot("sb", "yo", [128, 16, 64], BF16, 4)
        t1r = ph.rot("sb", "t1", [128, 16, 64], F32, 4)
        sttmp = ph.rot("sb", "sttmp", [128, 512], F32, 4)
        pseg2 = [ph.ps("pseg%d" % i_, [128, 4, 128]) for i_ in range(2)]
        pcbr = ph.rot("ps", "pcb", [128, 512], F32, 2)
        pyi2 = [ph.ps("pyi%d" % i_, [128, 8, 64]) for i_ in range(2)]
        pyn2 = [ph.ps("pyn%d" % i_, [128, 8, 64]) for i_ in range(2)]
        raw = ph.sb("dtraw", [128, NT, 32], F32)
        pre = ph.sb("pre", [128, NT, 2, 2, 16], F32)
        self.ld(raw[:], S["sdt"].ap.rearrange("(i p) c -> p i c", p=128), S["sdt"], raw)
        for d_ in range(2):
            hs_ = slice(d_ * 16, (d_ + 1) * 16)
            self.tt("dve", pre[:, :, d_, 0, :], raw[:, :, hs_], dtb[:, hs_].unsqueeze(1).to_broadcast([128, NT, 16]), ALU.add,
                    [raw, dtb], [pre])
            self.act(pre[:, :, d_, 0, :], pre[:, :, d_, 0, :], AF.Exp, [pre], [pre])
            self.act(pre[:, :, d_, 0, :], pre[:, :, d_, 0, :], AF.Ln, [pre], [pre], bias=1.0)
            self.tt("dve", pre[:, :, d_, 1, :], pre[:, :, d_, 0, :], negA[:, hs_].unsqueeze(1).to_broadcast([128, NT, 16]), ALU.mult,
                    [pre, negA], [pre])

        def run_dir(d):
            xcr = ph.rot("sb", "xc%d" % d, [128, 16, 64], BF16, 3)
            bcr = ph.rot("sb", "bc%d" % d, [128, 128], BF16, 3)
            St = Sts[d]; Sb = Sbs[d]
            M = C_U if d == 0 else C_L
            Ms = C_LS if d == 0 else C_US
            self.memset("dve", St[:], 0.0, [St])
            self.memset("dve", Sb[:], 0.0, [Sb])
            ydst = S["ys_f"] if d == 0 else S["ys_b"]
            self.ssd_calls = getattr(self, "ssd_calls", 0) + (1 if d == 0 else 0)
            order = self.chunk_order(d)[:(self.nch if self.ssd_calls == 1 else self.nch1)]
            loaded = {}

            def issue(c):
                xc = xcr.next(); bc = bcr.next()
                self.ld(xc[:].rearrange("p h q -> p (h q)"), S["sx_tm"].ap[c * 128:(c + 1) * 128, :], S["sx_tm"], xc)
                self.ld(bc[:], S["sb_tm"].ap[c * 128:(c + 1) * 128, :], S["sb_tm"], bc)
                loaded[c] = (xc, bc)

            issue(order[0])
            for ci, c in enumerate(order):
                if ci + 1 < len(order):
                    issue(order[ci + 1])
                xc, bc = loaded.pop(c)
                tok = slice(c * 128, (c + 1) * 128)
                sm = smr.next()
                self.cp("dve", sm[:, 0:2, :], pre[:, c, d, :, :], [pre], [sm])
                if FINE_IL and (FINE_MASK >> 0) & 1:
                    yield
                xdt = xdtr.next()
                self.tt("dve", xdt[:], xc[:], sm[:, 0, :].unsqueeze(2).to_broadcast([128, 16, 64]), ALU.mult, [xc, sm], [xdt])
                ra = rhsA.next()
                self.tt("dve", ra[:], cst[:, M, :].unsqueeze(1).to_broadcast([128, 16, 128]),
                        sm[:, 1, :].unsqueeze(2).to_broadcast([128, 16, 128]), ALU.mult, [cst, sm], [ra])
                pcb = pcbr.next()
                self.mm(pcb[:, 0:16], cst[:, M, :], sm[:, 1, :], True, True, [cst, sm], [pcb])
                self.mm(pcb[:, 16:32], cst[:, Ms, :], sm[:, 1, :], True, True, [cst, sm], [pcb])
                self.mm(pcb[:, 32:48], cst[:, C_ONE, :], sm[:, 1, :], True, True, [cst, sm], [pcb])
                for g in range(2):
                    self.mm(pcb[:, 256 + g * 128:256 + (g + 1) * 128], BT[:, g, tok], CT[:, g, tok], True, True, [BT, CT], [pcb])
                self.act(sm[:, 2:5, :], pcb[:, 0:48].rearrange("p (a h) -> p a h", h=16), AF.Exp, [pcb], [sm])
                cbm = CBm.next()
                self.tt("dve", cbm[:], pcb[:, 256:512].rearrange("p (g t) -> p g t", g=2),
                        cstb[:, M, :].unsqueeze(1).to_broadcast([128, 2, 128]), ALU.mult, [pcb, cstb], [cbm])
                if FINE_IL and (FINE_MASK >> 1) & 1:
                    yield
                dm = Dm.next()
                for g in range(2):
                    for q in range(2):
                        h0 = g * 8 + q * 4
                        self.mm(pseg2[q][:], cst[:, Ms, :], ra[:, h0:h0 + 4, :], True, True, [cst, ra], [pseg2[q]])
                    for q in range(2):
                        h0 = g * 8 + q * 4
                        self.act(dm[:, h0:h0 + 4, :], pseg2[q][:], AF.Exp, [pseg2[q]], [dm])
                if FINE_IL and (FINE_MASK >> 2) & 1:
                    yield
                sT = sTr.next()
                for g in range(2):
                    self.tt("dve" if g == 0 else "pool", sT[:, g * 8:(g + 1) * 8, :], dm[:, g * 8:(g + 1) * 8, :],
                            cbm[:, g:g + 1, :].to_broadcast([128, 8, 128]), ALU.mult, [dm, cbm], [sT])
                yint = yintr.next()
                for hb in range(2):
                    for h in range(hb * 8, (hb + 1) * 8):
                        self.mm(pyi2[hb][:, h - hb * 8, :], sT[:, h, :], xdt[:, h, :], True, True, [sT, xdt], [pyi2[hb]])
                    self.cp("act", yint[:, hb * 8:(hb + 1) * 8, :], pyi2[hb][:], [pyi2[hb]], [yint])
                if FINE_IL and (FINE_MASK >> 3) & 1:
                    yield
                t1 = t1r.next()
                for g in range(2):
                    self.mm(pyn2[g][:], CT[:, g, tok], Sb[:, g * 8:(g + 1) * 8, :], True, True, [CT, Sb], [pyn2[g]])
                    self.tt("dve", t1[:, g * 8:(g + 1) * 8, :], pyn2[g][:], sm[:, 2, g * 8:(g + 1) * 8].unsqueeze(2).to_broadcast([128, 8, 64]),
                            ALU.mult, [pyn2[g], sm], [t1])
                yo = yor.next()
                self.tt("dve", yo[:], t1[:], yint[:], ALU.add, [t1, yint], [yo])
                self.stq(ydst.ap[tok, :], yo[:].rearrange("p h q -> p (h q)"), yo, ydst)
                if FINE_IL and (FINE_MASK >> 4) & 1:
                    yield
                xh = xhr.next()
                self.tt("dve", xh[:], xdt[:], sm[:, 3, :].unsqueeze(2).to_broadcast([128, 16, 64]), ALU.mult, [xdt, sm], [xh])
                if FINE_IL and (FINE_MASK >> 5) & 1:
                    yield
                for g in range(2):
                    hs = slice(g * 8, (g + 1) * 8)
                    pstu = pseg2[g]
                    pstv = pstu[:].rearrange("p a b -> p (a b)")
                    self.mm(pstv[0:64, :], bc[:, g * 64:(g + 1) * 64], xh[:, hs, :], True, True, [bc, xh], [pstu])
                    if os.environ.get("ST_ALT", "0") == "1":
                        stt_ = sttmp.next()
                        self.act(stt_[0:64, :], pstu[0:64, :], AF.Copy, [pstu], [stt_])
                        self.tt("pool", St[:, hs, :], St[:, hs, :], sm[0:64, 4, hs].unsqueeze(2).to_broadcast([64, 8, 64]),
                                ALU.mult, [St, sm], [St])
                        self.tt("pool", St[:, hs, :], St[:, hs, :], stt_[0:64, :].rearrange("p (h q) -> p h q", q=64), ALU.add,
                                [St, stt_], [St])
                    else:
                        self.tt("dve", St[:, hs, :], St[:, hs, :], sm[0:64, 4, hs].unsqueeze(2).to_broadcast([64, 8, 64]),
                                ALU.mult, [St, sm], [St])
                        self.tt("dve", St[:, hs, :], St[:, hs, :], pstv[0:64, :].rearrange("p (h q) -> p h q", q=64), ALU.add,
                                [St, pstu], [St])
                if FINE_IL and (FINE_MASK >> 6) & 1:
                    yield
                self.cp("act", Sb[:], St[:], [St], [Sb])
                yield
        self.interleave([run_dir(0), run_dir(1)])
        ph.close()

    def ph_ml(self, l):
        I, S, P = self.I, self.S, self.P
        ph = self.phase()
        cst, cstb = self.cst, self.cstb
        qT = ph.sb("qT", [128, 4, T], BF16)
        kT = ph.sb("kT", [128, 4, T], BF16)
        self.ld(qT[:], S["mq_fm"].ap.rearrange("(h d) t -> d h t", d=128), S["mq_fm"], qT)
        self.ld(kT[:], S["mk_fm"].ap.rearrange("(h d) t -> d h t", d=128), S["mk_fm"], kT)
        bi = self.row_bc(ph, "bi", I["mib"][l], 8)
        bfb = self.row_bc(ph, "bfb", I["mfb"][l], 8)
        mneg = ph.sb("mneg", [128, 2, 4, 128], F32)
        for d in range(2):
            M = C_U if d == 0 else C_L
            self.ts("dve", mneg[:, d, :, :], cst[:, M, :].unsqueeze(1).to_broadcast([128, 4, 128]), 30000.0, -30000.0,
                    ALU.mult, ALU.add, [cst], [mneg])
        Css = [ph.sb("Cs%d" % d_, [128, 4, 257], F32) for d_ in range(2)]
        Cbs = [ph.sb("Cb%d" % d_, [128, 4, 257], BF16) for d_ in range(2)]
        smr = ph.rot("sb", "sm", [128, 8, 4], F32, 4)
        rhsA = ph.rot("sb", "rhsA", [128, 4, 128], F32, 3)
        libc = ph.rot("sb", "libc", [128, 4, 128], F32, 3)
        Dm = ph.rot("sb", "Dm", [128, 4, 128], F32, 3)
        Eb = ph.rot("sb", "Eb", [128, 4, 128], F32, 3)
        qtr = ph.rot("sb", "qt", [128, 4, 128], BF16, 4)
        sTr = ph.rot("sb", "sT", [128, 4, 128], BF16, 4)
        khr = ph.rot("sb", "kh", [128, 4, 128], BF16, 4)
        yor = ph.rot("sb", "yo", [128, 4, 256], BF16, 3)
        pseg = ph.ps("pseg", [128, 4, 128])
        pE = ph.ps("pE", [128, 4, 128])
        psc = ph.ps("psc", [128, 4, 128])
        pnum = [ph.ps("pnum%d" % h, [128, 512]) for h in range(4)]
        pst = ph.ps("pst", [128, 512])
        graw = ph.sb("graw", [128, NT, 16], F32)
        prem = ph.sb("prem", [128, NT, 2, 2, 4], F32)
        self.ld(graw[:], S["mg"].ap.rearrange("(i p) c -> p i c", p=128), S["mg"], graw)
        for d_ in range(2):
            hs_ = slice(d_ * 4, (d_ + 1) * 4)
            self.stt("dve", prem[:, :, d_, 0, :], graw[:, :, hs_], LN_ISQ, bi[:, hs_].unsqueeze(1).to_broadcast([128, NT, 4]),
                     ALU.add, ALU.add, [graw, bi], [prem])
            self.tt("dve", prem[:, :, d_, 1, :], graw[:, :, 8 + d_ * 4:8 + (d_ + 1) * 4],
                    bfb[:, hs_].unsqueeze(1).to_broadcast([128, NT, 4]), ALU.add, [graw, bfb], [prem])
            self.act(prem[:, :, d_, 1, :], prem[:, :, d_, 1, :], AF.Exp, [prem], [prem], scale=-1.0)
            self.act(prem[:, :, d_, 1, :], prem[:, :, d_, 1, :], AF.Ln, [prem], [prem], bias=1.0)

        def run_dir(d):
            kcr = ph.rot("sb", "kc%d" % d, [128, 4, 128], BF16, 3)
            vcr = ph.rot("sb", "vc%d" % d, [128, 4, 257], BF16, 3)
            for vb in vcr.bufs:
                self.memset("dve", vb[:], 1.0, [vb])
            Cs = Css[d]; Cb = Cbs[d]
            M = C_U if d == 0 else C_L
            Ms = C_LS if d == 0 else C_US
            self.memset("dve", Cs[:], 0.0, [Cs])
            self.memset("dve", Cb[:], 0.0, [Cb])
            ydst = S["ym_f"] if d == 0 else S["ym_b"]
            order = self.chunk_order(d)
            loaded = {}

            def issue(c):
                kc = kcr.next(); vc = vcr.next()
                self.ld(kc[:].rearrange("p h q -> p (h q)"), S["mk_tm"].ap[c * 128:(c + 1) * 128, :], S["mk_tm"], kc)
                self.ld(vc[:, :, 0:256], S["mv_tm"].ap[c * 128:(c + 1) * 128, :].rearrange("p (h q) -> p h q", q=256), S["mv_tm"], vc)
                loaded[c] = (kc, vc)

            issue(order[0])
            for ci, c in enumerate(order):
                if ci + 1 < len(order):
                    issue(order[ci + 1])
                kc, vc = loaded.pop(c)
                tok = slice(c * 128, (c + 1) * 128)
                sm = smr.next()
                self.cp("dve", sm[:, 0:2, :], prem[:, c, d, :, :], [prem], [sm])
                ra = rhsA.next()
                self.stt("dve", ra[:], cst[:, M, :].unsqueeze(1).to_broadcast([128, 4, 128]), -1.0,
                         sm[:, 1, :].unsqueeze(2).to_broadcast([128, 4, 128]), ALU.mult, ALU.mult, [cst, sm], [ra])
                lb = libc.next()
                self.tt("pool", lb[:], mneg[:, d, :, :], sm[:, 0, :].unsqueeze(2).to_broadcast([128, 4, 128]), ALU.add, [mneg, sm], [lb])
                if FINE_IL:
                    yield
                self.mm(pseg[:].rearrange("p h t -> p (h t)"), cst[:, Ms, :], ra[:].rearrange("p h t -> p (h t)"), True, False, [cst, ra], [pseg])
                self.mm(pseg[:].rearrange("p h t -> p (h t)"), cst[:, C_ID, :], lb[:].rearrange("p h t -> p (h t)"), False, True, [cst, lb], [pseg])
                self.mm(pE[:].rearrange("p h t -> p (h t)"), cst[:, C_ONE, :], ra[:].rearrange("p h t -> p (h t)"), True, True, [cst, ra], [pE])
                dm = Dm.next()
                self.act(dm[:], pseg[:], AF.Exp, [pseg], [dm])
                eb = Eb.next()
                self.act(eb[:], pE[:], AF.Exp, [pE], [eb])
                if FINE_IL:
                    yield
                qt = qtr.next()
                self.tt("pool", qt[:], qT[:, :, tok], eb[:], ALU.mult, [qT, eb], [qt])
                for h in range(4):
                    self.mm(psc[:, h, :], kT[:, h, tok], qT[:, h, tok], True, True, [kT, qT], [psc])
                sT = sTr.next()
                self.tt("dve", sT[:], psc[:], dm[:], ALU.mult, [psc, dm], [sT])
                if FINE_IL:
                    yield
                for h in range(4):
                    self.mm(pnum[h][:, 0:257], sT[:, h, :], vc[:, h, :], True, False, [sT, vc], [pnum[h]])
                    self.mm(pnum[h][:, 0:257], qt[:, h, :], Cb[:, h, :], False, True, [qt, Cb], [pnum[h]])
                for h in range(4):
                    self.cp("act" if h % 2 == 0 else "dve", sm[:, 4, h:h + 1], pnum[h][:, 256:257], [pnum[h]], [sm])
                self.stt("dve", sm[:, 5, :], sm[:, 4, :], -1.0, sm[:, 4, :], ALU.mult, ALU.max, [sm], [sm])
                self.ts("dve", sm[:, 5, :], sm[:, 5, :], 1.0, None, ALU.max, None, [sm], [sm])
                self.P.op("dve", lambda e, o=sm[:, 6, :], i_=sm[:, 5, :]: e.reciprocal(out=o, in_=i_), [sm], [sm])
                yo = yor.next()
                for h in range(4):
                    if h % 2 == 0:
                        self.act(yo[:, h, :], pnum[h][:, 0:256], AF.Copy, [pnum[h], sm], [yo], scale=sm[:, 6, h:h + 1])
                    else:
                        self.ts("dve", yo[:, h, :], pnum[h][:, 0:256], sm[:, 6, h:h + 1], None, ALU.mult, None, [pnum[h], sm], [yo])
                self.stq(ydst.ap[tok, :], yo[:].rearrange("p h q -> p (h q)"), yo, ydst)
                if FINE_IL:
                    yield
                self.mm(pst[:, 0:4], cst[:, Ms, :], sm[:, 1, :], True, True, [cst, sm], [pst])
                self.mm(pst[:, 4:8], cst[:, C_ONE, :], sm[:, 1, :], True, True, [cst, sm], [pst])
                self.tt("dve", sm[:, 2, :], sm[:, 0, :], pst[:, 0:4], ALU.subtract, [sm, pst], [sm])
                self.act(sm[:, 2, :], sm[:, 2, :], AF.Exp, [sm], [sm])
                self.act(sm[:, 3, :], pst[:, 4:8], AF.Exp, [pst], [sm], scale=-1.0)
                kh = khr.next()
                self.tt("pool", kh[:], kc[:], sm[:, 2, :].unsqueeze(2).to_broadcast([128, 4, 128]), ALU.mult, [kc, sm], [kh])
                for h in range(4):
                    pb_ = pst if h % 2 == 0 else pE
                    pv_ = pst[:, 0:257] if h % 2 == 0 else pE[:].rearrange("p h t -> p (h t)")[:, 0:257]
                    self.mm(pv_, kh[:, h, :], vc[:, h, :], True, True, [kh, vc], [pb_])
                    self.stt("dve", Cs[:, h, :], Cs[:, h, :], sm[:, 3, h:h + 1], pv_, ALU.mult, ALU.add, [Cs, sm, pb_], [Cs])
                self.cp("act", Cb[:], Cs[:], [Cs], [Cb])
                yield
        self.interleave([run_dir(0), run_dir(1)])
        ph.close()

    def gla_rows(self, dram_ap, c):
        if c < NCTX:
            return [(slice(0, 128), dram_ap[c * 128:(c + 1) * 128, :])]
        col0 = (c - NCTX) * 4
        lat = dram_ap[256:T, :].rearrange("(r c) f -> c r f", c=64)
        return [(slice(j * 32, (j + 1) * 32), lat[col0 + j]) for j in range(4)]

    def ph_gla(self, l):
        I, S, P = self.I, self.S, self.P
        self.wph = self.phase()
        self.Wb = [self.wph.sb(n, [128, 8, D], BF16) for n in ("w_b_ssd", "w_b_ml", "w_b_gla")]
        ph = self.phase()
        bstg = ph.rot("sb", "bstg", [128, 8, 512], F32, 1)

        def bg():
            for wb_, n in zip(self.Wb, ("w_b_ssd", "w_b_ml", "w_b_gla")):
                for c0 in range(0, D, 512):
                    st = bstg.next()
                    self.ld(st[:], I[n][l].rearrange("(k p) n -> p k n", p=128)[:, :, c0:c0 + 512], None, st)
                    for _ in range(6):
                        yield
                    self.cp("pool", wb_[:, :, c0:c0 + 512], st[:], [st], [wb_])
                    for _ in range(3):
                        yield
        cst, cstb = self.cst, self.cstb
        qT = ph.sb("qT", [128, 4, T], BF16)
        kT = ph.sb("kT", [128, 4, T], BF16)
        self.ld(qT[:], S["gq_fm"].ap.rearrange("(h d) t -> d h t", d=128), S["gq_fm"], qT)
        self.ld(kT[:], S["gk_fm"].ap.rearrange("(h d) t -> d h t", d=128), S["gk_fm"], kT)
        aT = [ph.sb("aT%d" % d, [17, T], BF16) for d in range(2)]
        for d in range(2):
            self.memset("dve", aT[d][:], 1.0, [aT[d]])
            self.ld(aT[d][0:16, :], S["ga_fm"].ap[d * 16:(d + 1) * 16, :], S["ga_fm"], aT[d])
        aupf = ph.sb("aupf", [17, 2, 512], F32)
        aupb = ph.sb("aupb", [17, 2, 512], BF16)
        self.ld(aupf[:], I["aup"][l].rearrange("d r n -> r d n"), None, aupf)
        self.cp("dve", aupb[:], aupf[:], [aupf], [aupb])
        Sss = [ph.sb("Ss%d" % d_, [128, 4, 256], F32) for d_ in range(2)]
        Sbfs = [ph.sb("Sbf%d" % d_, [128, 4, 256], BF16) for d_ in range(2)]
        nlgr = ph.rot("sb", "nlg", [128, 512], F32, 2)
        eqr = ph.rot("sb", "eq", [128, 4, 128], F32, 2)
        ekr = ph.rot("sb", "ek", [128, 4, 128], F32, 2)
        qtr = ph.rot("sb", "qt", [128, 4, 128], BF16, 3)
        ktr = ph.rot("sb", "kt", [128, 4, 128], BF16, 3)
        sTr = ph.rot("sb", "sT", [128, 4, 128], BF16, 3)
        wkr = ph.rot("sb", "wk", [128, 512], F32, 2)
        khr = ph.rot("sb", "kh", [128, 512], BF16, 3)
        elr = ph.rot("sb", "el", [128, 4], F32, 4)
        yor = ph.rot("sb", "yo", [128, 4, 256], BF16, 3)
        plg = ph.ps("plg", [128, 512])
        pbT = ph.ps("pbT", [128, 4, 128])
        psc = ph.ps("psc", [128, 4, 128])
        po2 = [ph.ps("po%d" % i_, [128, 2, 256]) for i_ in range(2)]
        pdl = ph.ps("pdl", [128, 512])
        pstp2 = [ph.ps("pstp%d" % i_, [128, 2, 256]) for i_ in range(2)]
        def run_dir(d):
            kcr = ph.rot("sb", "kc%d" % d, [128, 512], BF16, 3)
            vcr = ph.rot("sb", "vc%d" % d, [128, 4, 256], BF16, 3)
            Ss = Sss[d]; Sbf = Sbfs[d]
            M = C_U if d == 0 else C_L
            Ms = C_LS if d == 0 else C_US
            last = 127 if d == 0 else 0
            self.memset("dve", Ss[:], 0.0, [Ss])
            self.memset("dve", Sbf[:], 0.0, [Sbf])
            ydst = S["yg_f"] if d == 0 else S["yg_b"]
            order = self.chunk_order(d)
            loaded = {}

            def issue(c):
                kc = kcr.next(); vc = vcr.next()
                for (ps_, ap) in self.gla_rows(S["gk_tm"].ap, c):
                    self.ld(kc[ps_, :], ap, S["gk_tm"], kc)
                for (ps_, ap) in self.gla_rows(S["gv_tm"].ap, c):
                    self.ld(vc[ps_, :, :].rearrange("p h q -> p (h q)"), ap, S["gv_tm"], vc)
                loaded[c] = (kc, vc)

            issue(order[0])
            for ci, c in enumerate(order):
                if ci + 1 < len(order):
                    issue(order[ci + 1])
                kc, vc = loaded.pop(c)
                tok = slice(c * 128, (c + 1) * 128)
                self.mm(plg[:, :], aT[d][:, tok], aupb[:, d, :], True, True, [aT[d], aupb], [plg])
                nlg = nlgr.next()
                self.act(nlg[:], plg[:], AF.Exp, [plg], [nlg], scale=-1.0)
                self.act(nlg[:], nlg[:], AF.Ln, [nlg], [nlg], bias=1.0)
                if FINE_IL:
                    yield
                for h in range(4):
                    self.mm(pbT[:, h, :], nlg[:, h * 128:(h + 1) * 128], cst[:, M, :], True, True, [nlg, cst], [pbT])
                eq = eqr.next(); ek = ekr.next()
                self.act(eq[:], pbT[:], AF.Exp, [pbT], [eq], scale=-1.0 / 16.0)
                self.act(ek[:], pbT[:], AF.Exp, [pbT], [ek], scale=1.0 / 16.0)
                el = elr.next()
                self.cp("dve", el[:], eq[:, :, last], [eq], [el])
                if FINE_IL:
                    yield
                qt = qtr.next(); kt = ktr.next()
                self.stt("dve", qt[:], qT[:, :, tok], 128.0 ** -0.5, eq[:], ALU.mult, ALU.mult, [qT, eq], [qt])
                self.tt("pool", kt[:], kT[:, :, tok], ek[:], ALU.mult, [kT, ek], [kt])
                for h in range(4):
                    self.mm(psc[:, h, :], kt[:, h, :], qt[:, h, :], True, True, [kt, qt], [psc])
                sT = sTr.next()
                self.tt("dve", sT[:], psc[:], cst[:, M, :].unsqueeze(1).to_broadcast([128, 4, 128]), ALU.mult, [psc, cst], [sT])
                if FINE_IL:
                    yield
                for h in range(4):
                    pq = po2[h // 2]
                    self.mm(pq[:, h % 2, :], sT[:, h, :], vc[:, h, :], True, False, [sT, vc], [pq])
                    self.mm(pq[:, h % 2, :], qt[:, h, :], Sbf[:, h, :], False, True, [qt, Sbf], [pq])
                yo = yor.next()
                self.cp("act", yo[:, 0:2, :], po2[0][:], [po2[0]], [yo])
                self.cp("dve", yo[:, 2:4, :], po2[1][:], [po2[1]], [yo])
                for (ps_, ap) in self.gla_rows(ydst.ap, c):
                    self.stq(ap, yo[ps_, :, :].rearrange("p h q -> p (h q)"), yo, ydst)
                if FINE_IL:
                    yield
                self.mm(pdl[:, :], cst[:, Ms, :], nlg[:, :], True, True, [cst, nlg], [pdl])
                wk = wkr.next()
                self.act(wk[:], pdl[:], AF.Exp, [pdl], [wk], scale=-1.0 / 16.0)
                kh = khr.next()
                self.tt("pool", kh[:], kc[:], wk[:], ALU.mult, [kc, wk], [kh])
                for h in (0, 2, 1, 3):
                    pq = pstp2[h // 2]
                    self.mm(pq[:, h % 2, :], kh[:, h * 128:(h + 1) * 128], vc[:, h, :], True, True, [kh, vc], [pq])
                    self.stt("dve", Ss[:, h, :], Ss[:, h, :], el[:, h:h + 1], pq[:, h % 2, :], ALU.mult, ALU.add, [Ss, el, pq], [Ss])
                self.cp("act", Sbf[:], Ss[:], [Ss], [Sbf])
                yield
        self.interleave([run_dir(0), run_dir(1), bg()])
        ph.close()

    def load_resident_w(self, ph, stg, name, w_ap, kchunks, ncols):
        wb = ph.sb(name, [128, kchunks, ncols], BF16)
        for c0 in range(0, ncols, 512):
            for k0 in range(0, kchunks, 8):
                kn = min(8, kchunks - k0)
                st = stg.next()
                self.ld(st[:, 0:kn, :], w_ap.rearrange("(k p) n -> p k n", p=128)[:, k0:k0 + kn, c0:c0 + 512], None, st)
                self.cp("act", wb[:, k0:k0 + kn, c0:c0 + 512], st[:, 0:kn, :], [st], [wb])
        return wb

    def group_rstd(self, y, ng, gs, junk, ss):
        self.memset("pool", ss[:], 0.0, [ss])
        for g in range(ng):
            self.act(junk[:, 0:gs], y[:, g * gs:(g + 1) * gs], AF.Square, [y, ss], [junk, ss], accum_out=ss[:, g:g + 1])
        self.act(ss[:, ng:2 * ng], ss[:, 0:ng], AF.Ln, [ss], [ss], scale=1.0 / gs, bias=EPS)
        self.act(ss[:, ng:2 * ng], ss[:, ng:2 * ng], AF.Exp, [ss], [ss], scale=-0.5)

    def ph_epi(self, l):
        I, S, P = self.I, self.S, self.P
        last = (l == NL - 1)
        ph = self.phase()
        cst, cstb = self.cst, self.cstb
        wout = ph.sb("w_out", [128, 8, D], BF16)
        Wb = list(self.Wb) + [wout]
        inner = self.phase()
        stg = inner.rot("sb", "wstg", [128, 8, 512], F32, 2)
        for c0 in range(0, D, 512):
            st = stg.next()
            self.ld(st[:], I["w_out"][l].rearrange("(k p) n -> p k n", p=128)[:, :, c0:c0 + 512], None, st)
            self.cp("pool", wout[:, :, c0:c0 + 512], st[:], [st], [wout])
        inner.close()
        nw = []
        for n in ("snwT", "mnwT", "gnwT"):
            t = ph.sb(n, [128, 8], F32)
            self.ld(t[:], I[n][l], None, t)
            nw.append(t)
        dD = self.row_bc(ph, "dD", I["sd"][l], 16)
        yfr = ph.rot("sb", "yf", [128, D], BF16, 6)
        ybr = ph.rot("sb", "yb", [128, D], BF16, 6)
        a1r = ph.rot("sb", "a1", [128, D], BF16, 6)
        a2r = ph.rot("sb", "a2", [128, D], BF16, 2)
        gtr = ph.rot("sb", "gt", [128, 3 * D], BF16, 2)
        xtr = ph.rot("sb", "xt", [128, D], F32, 2)
        y1r = ph.rot("sb", "y1", [128, D], F32, 3)
        y2r = ph.rot("sb", "y2", [128, D], F32, 1)
        junkr = ph.rot("sb", "junk", [128, 512], BF16, 3)
        ssr = ph.rot("sb", "ss", [128, 8], F32, 4)
        ybf = ph.rot("sb", "ybf", [128, D], BF16, 7)
        yTr = ph.rot("sb", "yT", [128, 8, 128], BF16, 3)
        mrgr = ph.rot("sb", "mrg", [128, D], F32, 1)
        tmpr = ph.rot("sb", "tmp", [128, D], F32, 3)
        xn = ph.rot("sb", "xn", [128, D], F32, 1)
        ptr = ph.rot("ps", "pT", [128, 4, 128], F32, 2)
        ppr = ph.rot("ps", "pp", [128, 512], F32, 6)
        tiles = [i for i in range(NT) if not (last and i < NCTX)]
        BR = ((S["ys_f"], S["ys_b"], S["sx_tm"], 0, 2, 512),
              (S["ym_f"], S["ym_b"], S["mo_tm"], D, 4, 256),
              (S["yg_f"], S["yg_b"], S["gg_tm"], 2 * D, 4, 256))

        def issue_loads(i):
            rows = slice(i * 128, (i + 1) * 128)
            L = {}
            for bi_ in range(3):
                sf, sb_, sa = BR[bi_][0:3]
                yf = yfr.next(); yb = ybr.next(); a1 = a1r.next()
                self.ld(yf[:], sf.ap[rows, :], sf, yf)
                self.ld(yb[:], sb_.ap[rows, :], sb_, yb)
                self.ld(a1[:], sa.ap[rows, :], sa, a1)
                L[bi_] = (yf, yb, a1)
            L["a2"] = a2r.next(); self.ld(L["a2"][:], S["sz_tm"].ap[rows, :], S["sz_tm"], L["a2"])
            return L

        def issue_loads_b(i, L):
            rows = slice(i * 128, (i + 1) * 128)
            L["gt"] = gtr.next(); self.ld(L["gt"][:], S["gates"].ap[rows, :], S["gates"], L["gt"])
            L["xt"] = xtr.next(); self.ld(L["xt"][:], S["xs"].ap[rows, :], S["xs"], L["xt"])

        def transpose_proj_gen(src_bf, scale_t, W, out):
            yT = yTr.next()
            for kg in range(2):
                pt = ptr.next()
                for kk in range(4):
                    k = kg * 4 + kk
                    self.mm(pt[:, kk, :], src_bf[:, k * 128:(k + 1) * 128], cstb[:, C_ID, :], True, True, [src_bf, cstb], [pt])
                eng = self.evac_eng()
                for kk in range(4):
                    k = kg * 4 + kk
                    if scale_t is None:
                        self.cp(eng, yT[:, k, :], pt[:, kk, :], [pt], [yT])
                    else:
                        if eng == "act":
                            self.act(yT[:, k, :], pt[:, kk, :], AF.Copy, [pt, scale_t], [yT], scale=scale_t[:, k:k + 1])
                        else:
                            self.ts("dve", yT[:, k, :], pt[:, kk, :], scale_t[:, k:k + 1], None, ALU.mult, None, [pt, scale_t], [yT])
                yield
            pps = []
            for half in range(2):
                pp = ppr.next()
                for k in range(8):
                    self.mm(pp[:, :], yT[:, k, :], W[:, k, half * 512:(half + 1) * 512], k == 0, k == 7, [yT, W], [pp])
                pps.append(pp)
            out["pp"] = pps

        def prep(bi_, L, out):
            goff, ng, gs = BR[bi_][3:6]
            yf, yb, a1 = L[bi_]
            y1 = y1r.next()
            if bi_ == 0:
                a2 = L["a2"]; y2 = y2r.next()
            self.tt("pool", y1[:], yf[:], yb[:], ALU.add, [yf, yb], [y1])
            if bi_ == 0:
                self.tt("dve", y2[:].rearrange("p (h q) -> p h q", q=64), a1[:].rearrange("p (h q) -> p h q", q=64),
                        dD[:].unsqueeze(2).to_broadcast([128, 16, 64]), ALU.mult, [a1, dD], [y2])
                yield
                self.tt("pool", y1[:], y1[:], y2[:], ALU.add, [y1, y2], [y1])
                self.tt("dve", y1[:], y1[:], a2[:], ALU.mult, [y1, a2], [y1])
            yield
            ss = ssr.next()
            self.group_rstd(y1, ng, gs, junkr.next(), ss)
            yield
            yb16 = ybf.next()
            for g in range(ng):
                if bi_ == 0:
                    self.ts("dve", yb16[:, g * gs:(g + 1) * gs], y1[:, g * gs:(g + 1) * gs], ss[:, ng + g:ng + g + 1], None,
                            ALU.mult, None, [y1, ss], [yb16])
                else:
                    self.stt("dve", yb16[:, g * gs:(g + 1) * gs], y1[:, g * gs:(g + 1) * gs], ss[:, ng + g:ng + g + 1],
                             a1[:, g * gs:(g + 1) * gs], ALU.mult, ALU.mult, [y1, ss, a1], [yb16])
            out[bi_] = yb16
            yield

        def projb(bi_, yb16, gt, res):
            goff = BR[bi_][3]
            out = {}
            for _ in transpose_proj_gen(yb16, nw[bi_], Wb[bi_], out):
                yield
            pps = out["pp"]
            tmp = tmpr.next()
            for half in range(2):
                self.tt("dve", tmp[:, half * 512:(half + 1) * 512], pps[half][:, :],
                        gt[:, goff + half * 512:goff + (half + 1) * 512], ALU.mult, [pps[half], gt], [tmp])
            res[bi_] = tmp
            yield

        n_t = len(tiles)
        Lq = {}
        for ti in range(min(2, n_t)):
            Lq[ti] = issue_loads(tiles[ti])
        issue_loads_b(tiles[0], Lq[0])
        Pq = {0: {}}
        self.interleave([prep(b_, Lq[0], Pq[0]) for b_ in range(3)])
        for ti, i in enumerate(tiles):
            wv = 1 if i < NCTX else 0
            rows = slice(i * 128, (i + 1) * 128)
            if ti + 2 < n_t:
                Lq[ti + 2] = issue_loads(tiles[ti + 2])
            if ti + 1 < n_t:
                issue_loads_b(tiles[ti + 1], Lq[ti + 1])
            L = Lq.pop(ti)
            Pi = Pq.pop(ti)
            gt = L["gt"]; xt = L["xt"]
            res = {}
            gens = [projb(b_, Pi[b_], gt, res) for b_ in range(3)]
            if ti + 1 < n_t:
                Pq[ti + 1] = {}
                gens += [prep(b_, Lq[ti + 1], Pq[ti + 1]) for b_ in range(3)]
            self.interleave(gens)
            mrg = mrgr.next()
            self.tt("pool", mrg[:], res[0][:], res[1][:], ALU.add, [res[0], res[1]], [mrg])
            self.tt("pool", mrg[:], mrg[:], res[2][:], ALU.add, [mrg, res[2]], [mrg])
            mb16 = ybf.next()
            self.cp("act", mb16[:], mrg[:], [mrg], [mb16])
            out = {}
            for _ in transpose_proj_gen(mb16, None, Wb[3], out):
                pass
            pps = out["pp"]
            G1 = self.G[0][wv]
            tmp = tmpr.next()
            for half in range(2):
                self.tt("dve", tmp[:, half * 512:(half + 1) * 512], pps[half][:, :], G1[:, half * 512:(half + 1) * 512],
                        ALU.mult, [pps[half], G1], [tmp])
            xo = xn.next()
            self.tt("pool", xo[:], tmp[:], xt[:], ALU.add, [tmp, xt], [xo])
            self.stq(S["xs"].ap[rows, :], xo[:], xo, S["xs"], q="act")
        ph.close()
        self.wph.close()

    def ph_ffn(self, l):
        I, S, P = self.I, self.S, self.P
        last = (l == NL - 1)
        tiles = [i for i in range(NT) if not (last and i < NCTX)]
        tok0 = tiles[0] * 128
        ntok = len(tiles) * 128
        wph2 = self.phase()
        wfo = wph2.sb("wfo", [128, 22, D], BF16)
        hph = self.phase()
        hT2 = hph.sb("hT2", [128, 8, ntok], BF16)
        ph = self.phase()
        self.norm_to_hT(ph, hT2, self.A2, 24, tiles, tok0)
        ph.close()
        ph = self.phase()
        stg = ph.rot("sb", "wstg", [128, 8, 512], F32, 3)
        wbr = ph.rot("sb", "wbf", [128, 8, 512], BF16, 4)
        psg = ph.rot("ps", "psg", [128, 512], F32, 3)
        psu = ph.rot("ps", "psu", [128, 512], F32, 3)
        sgr = ph.rot("sb", "sg", [128, 512], F32, 3)
        aor = ph.rot("sb", "ao", [128, ntok], BF16, 2)
        w_f = I["w_ffn_in"][l]
        tgs = [(t0, min(512, ntok - t0)) for t0 in range(0, ntok, 512)]
        def bgf():
            wv_ = I["w_ffn_out"][l].rearrange("(k p) n -> p k n", p=128)
            for c0_ in range(0, D, 512):
                for k0 in range(0, 22, 8):
                    kn = min(8, 22 - k0)
                    st = stg.next()
                    self.ld(st[:, 0:kn, :], wv_[:, k0:k0 + kn, c0_:c0_ + 512], None, st)
                    yield
                    yield
                    self.cp("pool", wfo[:, k0:k0 + kn, c0_:c0_ + 512], st[:, 0:kn, :], [st], [wfo])
                    yield

        bgen = bgf()
        fq = {}

        def ldf(c0_):
            nc_ = min(512, DFF - c0_)
            fq[c0_] = (self.load_w_bf(stg, wbr, w_f, c0_, nc_), self.load_w_bf(stg, wbr, w_f, DFF + c0_, nc_))

        ldf(0)
        for c0 in range(0, DFF, 512):
            ncols = min(512, DFF - c0)
            if c0 + 512 < DFF:
                ldf(c0 + 512)
            wg, wu = fq.pop(c0)
            for jj in range(ncols // 128):
                next(bgen, None)
                ao = aor.next()
                for (t0, tn) in tgs:
                    pg = psg.next(); pu = psu.next()
                    for k in range(8):
                        self.mm(pg[:, 0:tn], wg[:, k, jj * 128:(jj + 1) * 128], hT2[:, k, t0:t0 + tn], k == 0, k == 7, [wg, hT2], [pg])
                    for k in range(8):
                        self.mm(pu[:, 0:tn], wu[:, k, jj * 128:(jj + 1) * 128], hT2[:, k, t0:t0 + tn], k == 0, k == 7, [wu, hT2], [pu])
                    sg = sgr.next()
                    self.act(sg[:, 0:tn], pg[:, 0:tn], AF.Silu, [pg], [sg])
                    self.tt("dve", ao[:, t0:t0 + tn], sg[:, 0:tn], pu[:, 0:tn], ALU.mult, [sg, pu], [ao])
                r0 = c0 + jj * 128
                self.stq(S["aT"].ap[r0:r0 + 128, tok0:tok0 + ntok], ao[:], ao, S["aT"])
        for _ in bgen:
            pass
        ph.close()
        hph.close()
        ph = self.phase()
        abr = ph.rot("sb", "ab", [128, 22, 512], BF16, 2)
        xtr = ph.rot("sb", "xt", [128, D], F32, 3)
        tmp = ph.rot("sb", "tmp", [128, D], F32, 2)
        xnr = ph.rot("sb", "xn", [128, D], F32, 2)
        outr = ph.rot("sb", "oo", [128, D], F32, 2)
        junk = ph.sb("junk", [128, D], F32)
        ssr = ph.rot("sb", "ss", [128, 2], F32, 3)
        ppr = ph.rot("ps", "pp", [128, 512], F32, 4)
        if last:
            fw = self.row_bc(ph, "fw", I["fnw"][0], D)
        for (t0, tn) in tgs:
            ab = abr.next()
            self.ld(ab[:, :, 0:tn], S["aT"].ap.rearrange("(j p) t -> p j t", p=128)[:, :, tok0 + t0:tok0 + t0 + tn], S["aT"], ab)
            for tt_ in range(tn // 128):
                i = (tok0 + t0) // 128 + tt_
                wv = 1 if i < NCTX else 0
                rows = slice(i * 128, (i + 1) * 128)
                xt = xtr.next()
                self.ld(xt[:], S["xs"].ap[rows, :], S["xs"], xt)
                G2 = self.G[1][wv]
                tm_ = tmp.next()
                for half in range(2):
                    pp = ppr.next()
                    for j in range(22):
                        self.mm(pp[:, :], ab[:, j, tt_ * 128:(tt_ + 1) * 128], wfo[:, j, half * 512:(half + 1) * 512],
                                j == 0, j == 21, [ab, wfo], [pp])
                    self.tt("dve", tm_[:, half * 512:(half + 1) * 512], pp[:, :], G2[:, half * 512:(half + 1) * 512],
                            ALU.mult, [pp, G2], [tm_])
                xo = xnr.next()
                self.tt("pool", xo[:], tm_[:], xt[:], ALU.add, [tm_, xt], [xo])
                if not last:
                    self.stq(S["xs"].ap[rows, :], xo[:], xo, S["xs"])
                else:
                    ss = ssr.next()
                    self.memset("pool", ss[:], 0.0, [ss])
                    self.act(junk[:], xo[:], AF.Square, [xo, ss], [junk, ss], accum_out=ss[:, 0:1])
                    self.act(ss[:, 1:2], ss[:, 0:1], AF.Ln, [ss], [ss], scale=1.0 / D, bias=EPS)
                    self.act(ss[:, 1:2], ss[:, 1:2], AF.Exp, [ss], [ss], scale=-0.5)
                    oo = outr.next()
                    self.stt("dve", oo[:], xo[:], ss[:, 1:2], fw[:], ALU.mult, ALU.mult, [xo, ss, fw], [oo])
                    self.stq(self.out_ap[(i - NCTX) * 128:(i - NCTX + 1) * 128, :], oo[:], oo, self.outb)
        ph.close()
        wph2.close()


def _consts():
    i = np.arange(128)[:, None]
    j = np.arange(128)[None, :]
    c = np.zeros((128, 6, 128), np.float32)
    c[:, C_ID] = (i == j)
    c[:, C_U] = (i <= j)
    c[:, C_L] = (i >= j)
    c[:, C_LS] = (i > j)
    c[:, C_US] = (i < j)
    c[:, C_ONE] = 1.0
    sel = np.zeros((2, 2, 128), np.float32)
    sel[0, 0] = 1.0
    sel[1, 1] = 1.0
    return c, sel


def _tl(v):
    L = v.shape[0]
    return np.ascontiguousarray(v.reshape(L, -1, 128).transpose(0, 2, 1))


def make_in_maps(inp):
    f = lambda a: np.ascontiguousarray(np.asarray(a, dtype=np.float32))
    consts, sel = _consts()
    shared = {
        "w_mod": f(inp["w_mod"]), "b_modT": _tl(f(inp["b_mod"])), "b_mod": f(inp["b_mod"]),
        "nmwT": _tl(f(inp["norm_mix_w"])), "nfwT": _tl(f(inp["norm_ffn_w"])),
        "fnw": f(inp["final_norm_w"]).reshape(1, D),
        "w_in": f(inp["w_in"]),
        "sconvT": np.ascontiguousarray(f(inp["ssd_conv_w"]).reshape(NL, 5, 10, 128).transpose(0, 3, 2, 1)),
        "sconvb": _tl(f(inp["ssd_conv_b"])),
        "mconvT": np.ascontiguousarray(f(inp["ml_conv_w"]).reshape(NL, 5, 8, 128).transpose(0, 3, 2, 1)),
        "mconvb": _tl(f(inp["ml_conv_b"])),
        "sdtb": f(inp["ssd_dt_bias"]).reshape(NL, 32), "salog": f(inp["ssd_a_log"]).reshape(NL, 32),
        "sd": f(inp["ssd_d"]), "snwT": _tl(f(inp["ssd_norm_w"])),
        "mib": f(inp["ml_i_bias"]).reshape(NL, 8), "mfb": f(inp["ml_f_bias"]).reshape(NL, 8),
        "mnwT": _tl(f(inp["ml_norm_w"])),
        "aup": np.ascontiguousarray(np.concatenate([f(inp["gla_a_up"]), f(inp["gla_a_bias"])[:, :, None, :]], axis=2)),
        "gnwT": _tl(f(inp["gla_norm_w"])),
        "w_b_ssd": f(inp["w_b_ssd"]), "w_b_ml": f(inp["w_b_ml"]), "w_b_gla": f(inp["w_b_gla"]), "w_out": f(inp["w_out"]),
        "w_ffn_in": f(inp["w_ffn_in"]), "w_ffn_out": f(inp["w_ffn_out"]),
        "consts": consts, "sel": sel,
    }
    x = f(inp["x"]); c = f(inp["c"]); ctx = f(inp["ctx"]); cc = f(inp["c_ctx"])
    maps = []
    for b in range(x.shape[0]):
        cvec = np.stack([c[b].reshape(8, 128).T, cc.reshape(8, 128).T], axis=-1)
        m = dict(shared)
        m["x"] = np.ascontiguousarray(x[b])
        m["ctx"] = np.ascontiguousarray(ctx[b])
        m["cvec"] = np.ascontiguousarray(cvec.astype(np.float32))
        maps.append(m)
    return maps


_NC_CACHE = {}
FUSED = True


def kernel(**inputs):
    maps = make_in_maps(inputs)
    if FUSED:
        if "nc" not in _NC_CACHE:
            _NC_CACHE["nc"] = Builder().build()
        res = run_bass_kernel_spmd(_NC_CACHE["nc"], maps, core_ids=list(range(8)))
        return np.stack([np.asarray(r["out"], dtype=np.float32) for r in res.results], axis=0)
    if "l0" not in _NC_CACHE:
        _NC_CACHE["l0"] = Builder(debug={"xs"}, layer_ids=[0]).build()
        _NC_CACHE["l1"] = Builder(layer_ids=[1]).build()
    res = run_bass_kernel_spmd(_NC_CACHE["l0"], maps, core_ids=list(range(8)))
    for m, r in zip(maps, res.results):
        xs = np.asarray(r["xs"], dtype=np.float32)
        m["ctx"] = np.ascontiguousarray(xs[:256])
        m["x"] = np.ascontiguousarray(xs[256:])
    res = run_bass_kernel_spmd(_NC_CACHE["l1"], maps, core_ids=list(range(8)))
    return np.stack([np.asarray(r["out"], dtype=np.float32) for r in res.results], axis=0)
```

```python
import contextlib
import math
import os
import numpy as np
import concourse.bass as bass
import concourse.mybir as mybir
from concourse.bass_utils import run_bass_kernel_spmd

F32 = mybir.dt.float32
BF16 = mybir.dt.bfloat16
AF = mybir.ActivationFunctionType
ALU = mybir.AluOpType

NL = 2
T = 2304
NT = 18
NCTX = 2
D = 1024
DIN = 11600
DFF = 2816
EPS = 1e-6
LN_ISQ = math.log(128.0 ** -0.5)
C_ID, C_U, C_L, C_LS, C_US, C_ONE = range(6)


class Buf:
    def __init__(self, name, t=None):
        self.name = name
        self.t = t
        self.last_w = None
        self.readers = {}
        self.dsem = None

    def __getitem__(self, idx):
        return self.t[idx]


import os as _os
POOL_TO = _os.environ.get("POOL_TO")
SEM_ROT = int(_os.environ.get("SEM_ROT", "6000"))
FINE_IL = _os.environ.get("FINE_IL", "1") == "1"
FINE_MASK = int(_os.environ.get("FINE_MASK", "255"))


class Prog:
    ENGS = ("pe", "dve", "act", "pool", "sp")

    def __init__(self, nc):
        self.nc = nc
        self.streams = {e: [] for e in self.ENGS}
        self.count = {}
        self.known = {e: {} for e in self.ENGS}
        self.semkeys = []
        self.free_dsems = []
        self.engkey = {}
        for e in ("pe", "dve", "act", "pool"):
            self.engkey[e] = self._newsem("eng_" + e)
        self.ninst = 0

    def _newsem(self, key):
        self.semkeys.append(key)
        self.count[key] = 0
        return key

    def buf(self, name, t=None):
        return Buf(name, t)

    def _dsem(self, b):
        if b.dsem is None:
            if self.free_dsems:
                b.dsem = self.free_dsems.pop()
            else:
                b.dsem = self._newsem("dma%d" % len(self.semkeys))
        return b.dsem

    def release(self, bufs):
        for b in bufs:
            if b.dsem is not None:
                self.free_dsems.append(b.dsem)
                b.dsem = None

    def _need(self, eng, ev, waits):
        if ev is None:
            return
        k, v = ev
        if self.known[eng].get(k, 0) >= v:
            return
        if waits.get(k, 0) < v:
            waits[k] = v

    def _deps(self, eng, reads, writes, skip_sem=None, same_eng_sync=True):
        waits = {}
        own = self.engkey.get(eng)
        for r in reads:
            self._need(eng, r.last_w, waits)
            if getattr(r, "psum", False):
                for k, v in r.readers.items():
                    if k != own:
                        self._need(eng, (k, v), waits)
        for w in writes:
            if w.last_w is not None and w.last_w[0] != skip_sem:
                self._need(eng, w.last_w, waits)
            for k, v in w.readers.items():
                if k != skip_sem:
                    self._need(eng, (k, v), waits)
        if not same_eng_sync:
            waits.pop(own, None)
        for k, v in waits.items():
            self.streams[eng].append(("wait", k, v))
            self.known[eng][k] = v

    def op(self, eng, fn, reads=(), writes=(), same_eng_sync=True):
        if eng == "pool" and POOL_TO is not None:
            eng = POOL_TO
        self._deps(eng, reads, writes, same_eng_sync=same_eng_sync)
        key = self.engkey[eng]
        self.count[key] += 1
        v = self.count[key]
        self.streams[eng].append(("ins", fn, key, 1))
        self.ninst += 1
        for r in reads:
            if r.readers.get(key, 0) < v:
                r.readers[key] = v
        for w in writes:
            w.last_w = (key, v)
            w.readers = {}
        return v

    def dma(self, q, out_ap, in_ap, src, dst, owner=None, **kw):
        if owner is None:
            owner = dst if (dst is not None and dst.t is not None) else src
        key = self._dsem(owner)
        reads = [src] if src is not None else []
        writes = [dst] if dst is not None else []
        self._deps(q, reads, writes, skip_sem=key)
        self.count[key] += 16
        v = self.count[key]

        def fn(e, out_ap=out_ap, in_ap=in_ap, kw=kw):
            return e.dma_start(out=out_ap, in_=in_ap, **kw)

        self.streams[q].append(("ins", fn, key, 16))
        self.ninst += 1
        if src is not None:
            if src.readers.get(key, 0) < v:
                src.readers[key] = v
        if dst is not None:
            dst.last_w = (key, v)
            dst.readers = {}
        return v

    def barrier(self):
        for e in self.ENGS:
            for k in self.semkeys:
                v = self.count[k]
                if v > self.known[e].get(k, 0):
                    self.streams[e].append(("wait", k, v))
                    self.known[e][k] = v
        for e in list(self.engkey):
            if self.count[self.engkey[e]] > SEM_ROT:
                self.engkey[e] = self._newsem("eng_%s_%d" % (e, len(self.semkeys)))

    def emit(self):
        nc = self.nc
        with contextlib.ExitStack() as st:
            sems = {}
            for k in self.semkeys:
                sems[k] = st.enter_context(nc.semaphore(k))
            block = st.enter_context(nc.Block())

            def replay(e, name):
                for it in self.streams[name]:
                    if it[0] == "wait":
                        e.wait_ge(sems[it[1]], it[2])
                    else:
                        _, fn, key, inc = it
                        fn(e).then_inc(sems[key], inc)

            @block.sync
            def _(e):
                replay(e, "sp")

            @block.tensor
            def _(e):
                replay(e, "pe")

            @block.vector
            def _(e):
                replay(e, "dve")

            @block.scalar
            def _(e):
                replay(e, "act")

            @block.gpsimd
            def _(e):
                replay(e, "pool")


class Rot:
    def __init__(self, bufs):
        self.bufs = bufs
        self.i = 0

    def next(self):
        b = self.bufs[self.i % len(self.bufs)]
        self.i += 1
        return b


class Builder:
    def __init__(self, debug=(), stop_after=None, layers=NL, layer_ids=None):
        self.layer_ids = layer_ids
        self.debug = set(debug)
        self.stop_after = stop_after
        self.layers = layers
        self.nc = bass.Bass("TRN2", target_bir_lowering=False)
        self.P = Prog(self.nc)
        self.uid = 0
        self.rr = 0
        self.nch = int(os.environ.get("SSD_NCH", "99"))
        self.nch1 = int(os.environ.get("SSD_NCH1", "99"))
        self.sub0 = int(os.environ.get("SSD_SUB", "99"))
        self.sub1 = int(os.environ.get("SSD_SUB1", "99"))
        self.sub = 99
        self.stop_occ = int(os.environ.get("STOP_OCC", "1"))

    def din(self, name, shape, dt=F32):
        return self.nc.dram_tensor(name, list(shape), dt, kind="ExternalInput").ap()

    def scr(self, name, shape, dt):
        kind = "ExternalOutput" if name in self.debug else "Internal"
        ap = self.nc.dram_tensor(name, list(shape), dt, kind=kind).ap()
        b = self.P.buf(name)
        b.ap = ap
        return b

    class Phase:
        def __init__(self, bld):
            self.b = bld
            self.st = contextlib.ExitStack()
            self.bufs = []

        def sb(self, name, shape, dt=F32):
            self.b.uid += 1
            t = self.st.enter_context(self.b.nc.sbuf_tensor("%s_%d" % (name, self.b.uid), list(shape), dt))
            bf = self.b.P.buf(name, t)
            self.bufs.append(bf)
            return bf

        def ps(self, name, shape, dt=F32):
            self.b.uid += 1
            t = self.st.enter_context(self.b.nc.psum_tensor("%s_%d" % (name, self.b.uid), list(shape), dt))
            bf = self.b.P.buf(name, t)
            bf.psum = True
            self.bufs.append(bf)
            return bf

        def rot(self, kind, name, shape, dt, n):
            f = self.sb if kind == "sb" else self.ps
            return Rot([f("%s%d" % (name, i), shape, dt) for i in range(n)])

        def close(self):
            self.b.P.barrier()
            self.b.P.release(self.bufs)
            self.st.close()

    def phase(self):
        return Builder.Phase(self)

    def mm(self, out, lhsT, rhs, start, stop, reads, writes):
        self.P.op("pe", lambda e: e.matmul(out, lhsT=lhsT, rhs=rhs, start=start, stop=stop),
                  reads, writes, same_eng_sync=False)

    def act(self, out, in_, func, reads, writes, **kw):
        self.P.op("act", lambda e: e.activation(out=out, in_=in_, func=func, **kw), reads, writes)

    def tt(self, eng, out, in0, in1, op, reads, writes):
        self.P.op(eng, lambda e: e.tensor_tensor(out=out, in0=in0, in1=in1, op=op), reads, writes)

    def ts(self, eng, out, in0, s1, s2, op0, op1, reads, writes):
        if s2 is None:
            self.P.op(eng, lambda e: e.tensor_scalar(out=out, in0=in0, scalar1=s1, scalar2=None, op0=op0), reads, writes)
        else:
            self.P.op(eng, lambda e: e.tensor_scalar(out=out, in0=in0, scalar1=s1, scalar2=s2, op0=op0, op1=op1), reads, writes)

    def stt(self, eng, out, in0, scalar, in1, op0, op1, reads, writes):
        eng = "dve"
        self.P.op(eng, lambda e: e.scalar_tensor_tensor(out=out, in0=in0, scalar=scalar, in1=in1, op0=op0, op1=op1),
                  reads, writes)

    def cp(self, eng, out, in_, reads, writes):
        if eng == "act":
            if os.environ.get("ACTCOPY", "act") == "act":
                self.P.op("act", lambda e: e.activation(out=out, in_=in_, func=AF.Copy), reads, writes)
            else:
                self.P.op("act", lambda e: e.copy(out=out, in_=in_), reads, writes)
        else:
            self.P.op(eng, lambda e: e.tensor_copy(out=out, in_=in_), reads, writes)

    def memset(self, eng, ap, val, writes):
        self.P.op(eng, lambda e: e.memset(ap, val), [], writes)

    def ld(self, out_ap, in_ap, src, dst, q="sp"):
        self.P.dma(q, out_ap, in_ap, src, dst)

    def stq(self, out_ap, in_ap, src, dst, q="sp"):
        self.P.dma(q, out_ap, in_ap, src, dst)

    def interleave(self, gens):
        gens = list(gens)
        while gens:
            for g in list(gens):
                try:
                    next(g)
                except StopIteration:
                    gens.remove(g)

    def evac_eng(self):
        self.rr += 1
        return "act" if self.rr % 2 else "dve"

    def build(self):
        nc, P = self.nc, self.P
        I = {}
        I["x"] = self.din("x", [2048, D])
        I["ctx"] = self.din("ctx", [256, D])
        I["cvec"] = self.din("cvec", [128, 8, 2])
        I["w_mod"] = self.din("w_mod", [NL, D, 6 * D])
        I["b_modT"] = self.din("b_modT", [NL, 128, 48])
        I["b_mod"] = self.din("b_mod", [NL, 6 * D])
        I["nmwT"] = self.din("nmwT", [NL, 128, 8])
        I["nfwT"] = self.din("nfwT", [NL, 128, 8])
        I["fnw"] = self.din("fnw", [1, D])
        I["w_in"] = self.din("w_in", [NL, D, DIN])
        I["sconvT"] = self.din("sconvT", [NL, 128, 10, 5])
        I["sconvb"] = self.din("sconvb", [NL, 128, 10])
        I["mconvT"] = self.din("mconvT", [NL, 128, 8, 5])
        I["mconvb"] = self.din("mconvb", [NL, 128, 8])
        I["sdtb"] = self.din("sdtb", [NL, 32])
        I["salog"] = self.din("salog", [NL, 32])
        I["sd"] = self.din("sd", [NL, 16])
        I["snwT"] = self.din("snwT", [NL, 128, 8])
        I["mib"] = self.din("mib", [NL, 8])
        I["mfb"] = self.din("mfb", [NL, 8])
        I["mnwT"] = self.din("mnwT", [NL, 128, 8])
        I["aup"] = self.din("aup", [NL, 2, 17, 512])
        I["gnwT"] = self.din("gnwT", [NL, 128, 8])
        for n in ("w_b_ssd", "w_b_ml", "w_b_gla", "w_out"):
            I[n] = self.din(n, [NL, D, D])
        I["w_ffn_in"] = self.din("w_ffn_in", [NL, D, 2 * DFF])
        I["w_ffn_out"] = self.din("w_ffn_out", [NL, DFF, D])
        I["consts"] = self.din("consts", [128, 6, 128])
        I["sel"] = self.din("sel", [2, 2, 128])
        self.I = I
        self.out_ap = nc.dram_tensor("out", [2048, D], F32, kind="ExternalOutput").ap()
        self.outb = P.buf("out")

        S = {}
        S["xs"] = self.scr("xs", [T, D], F32)
        S["sbc_fm"] = self.scr("sbc_fm", [256, T], BF16)
        S["sx_tm"] = self.scr("sx_tm", [T, D], BF16)
        S["sb_tm"] = self.scr("sb_tm", [T, 128], BF16)
        S["sz_tm"] = self.scr("sz_tm", [T, D], BF16)
        S["sdt"] = self.scr("sdt", [T, 32], F32)
        S["mq_fm"] = self.scr("mq_fm", [512, T], BF16)
        S["mk_fm"] = self.scr("mk_fm", [512, T], BF16)
        S["mk_tm"] = self.scr("mk_tm", [T, 512], BF16)
        S["mv_tm"] = self.scr("mv_tm", [T, D], BF16)
        S["mo_tm"] = self.scr("mo_tm", [T, D], BF16)
        S["mg"] = self.scr("mg", [T, 16], F32)
        S["gq_fm"] = self.scr("gq_fm", [512, T], BF16)
        S["gk_fm"] = self.scr("gk_fm", [512, T], BF16)
        S["gk_tm"] = self.scr("gk_tm", [T, 512], BF16)
        S["gv_tm"] = self.scr("gv_tm", [T, D], BF16)
        S["gg_tm"] = self.scr("gg_tm", [T, D], BF16)
        S["ga_fm"] = self.scr("ga_fm", [32, T], BF16)
        S["gates"] = self.scr("gates", [T, 3 * D], BF16)
        for n in ("ys_f", "ys_b", "ym_f", "ym_b", "yg_f", "yg_b"):
            S[n] = self.scr(n, [T, D], BF16)
        S["aT"] = self.scr("aT", [DFF, T], BF16)
        self.S = S

        glob = self.phase()
        self.glob = glob
        cst = glob.sb("cst", [128, 6, 128], F32)
        cstb = glob.sb("cstb", [128, 6, 128], BF16)
        sel = glob.sb("sel", [2, 2, 128], F32)
        scT = glob.sb("scT", [128, 8, 2], F32)
        self.cst, self.cstb, self.sel, self.scT = cst, cstb, sel, scT
        self.ld(cst[:], I["consts"], None, cst)
        self.ld(sel[:], I["sel"], None, sel)
        self.ld(scT[:], I["cvec"], None, scT)
        self.cp("dve", cstb[:], cst[:], [cst], [cstb])
        self.act(scT[:], scT[:], AF.Silu, [scT], [scT])
        dummy = P.buf("dummy")
        P.dma("sp", S["xs"].ap[0:256, :], I["ctx"], None, S["xs"], owner=dummy)
        P.dma("sp", S["xs"].ap[256:T, :], I["x"], None, S["xs"], owner=dummy)
        P.barrier()

        self.modT = glob.sb("modT", [128, 48, 2], F32)
        self.A1 = glob.sb("A1", [128, 8, 2], F32)
        self.A2 = glob.sb("A2", [128, 8, 2], F32)
        self.G = [[glob.sb("G%d%d" % (a, w), [128, D], F32) for w in range(2)] for a in range(2)]

        stages = ["mod", "n1", "proj", "ssd", "ml", "gla", "epi", "ffn"]
        ndummy = int(os.environ.get("NDUMMY", "0"))
        if ndummy:
            dph = self.phase()
            dps = dph.ps("dps", [128, 128])
            for _ in range(ndummy):
                self.mm(dps[:], cstb[:, C_ID, :], cstb[:, C_ID, :], True, True, [cstb], [dps])
            dph.close()
        done = False
        for l in (self.layer_ids if self.layer_ids is not None else range(self.layers)):
            for stg in stages:
                if l == 0 and stg in os.environ.get("SKIP0", "").split(","):
                    continue
                getattr(self, "ph_" + stg)(l)
                if stg == "ssd":
                    for _ in range(int(os.environ.get("REPEAT_SSD", "1")) - 1):
                        self.ph_ssd(l)
                if self.stop_after == (l, stg):
                    self.stop_occ -= 1
                    if self.stop_occ <= 0:
                        done = True
                        break
            if done:
                break
        P.barrier()
        glob.st.close()
        P.emit()
        return nc

    def wview(self, w_ap, c0, n):
        return w_ap.rearrange("(k p) n -> p k n", p=128)[:, :, c0:c0 + n]

    def ph_mod(self, l):
        I, P = self.I, self.P
        ph = self.phase()
        wst = ph.rot("sb", "wst", [128, 8, 512], F32, 3)
        pmf = ph.ps("pm", [128, 512])
        pm = pmf
        pmv = pmf.t[:, 0:96].rearrange("p (j w) -> p j w", w=2)
        prow = ph.rot("ps", "prow", [2, 512], F32, 2)
        pbc = ph.rot("ps", "pbc", [128, 512], F32, 2)
        brow = ph.sb("brow", [2, 6 * D], F32)
        rows = ph.sb("rows", [2, 6 * D], F32)
        nmw = ph.sb("nmw", [128, 8], F32)
        nfw = ph.sb("nfw", [128, 8], F32)
        self.ld(nmw[:], I["nmwT"][l], None, nmw)
        self.ld(nfw[:], I["nfwT"][l], None, nfw)
        self.ld(brow[:], I["b_mod"][l].partition_broadcast(2), None, brow)
        scT = self.scT
        cst = self.cst
        wq = {}

        def ldw(g):
            w_ = wst.next()
            self.ld(w_[:], self.wview(I["w_mod"][l], g * 512, 512), None, w_, q=("sp" if g % 2 == 0 else "act"))
            wq[g] = w_

        ldw(0)
        ldw(1)
        for g in range(12):
            if g + 2 < 12:
                ldw(g + 2)
            w = wq.pop(g)
            pr = prow.next()
            for k in range(8):
                self.mm(pr[:, :], scT[:, k, :], w[:, k, :], k == 0, k == 7, [w, scT], [pr])
            self.tt("dve", rows[:, g * 512:(g + 1) * 512], pr[:, :], brow[:, g * 512:(g + 1) * 512], ALU.add, [pr, brow], [rows])
        for j in range(48):
            self.mm(pmv[:, j, :], rows[:, j * 128:(j + 1) * 128], cst[0:2, C_ID, 0:2], True, True, [rows, cst], [pm])
        self.cp("dve", self.modT[:], pmv, [pm], [self.modT])
        modT = self.modT
        for wv in range(2):
            self.stt("dve", self.A1[:, :, wv], modT[:, 8:16, wv], 1.0, nmw[:], ALU.add, ALU.mult, [modT, nmw], [self.A1])
            self.stt("dve", self.A2[:, :, wv], modT[:, 32:40, wv], 1.0, nfw[:], ALU.add, ALU.mult, [modT, nfw], [self.A2])
        for a_, c0 in ((0, 2 * D), (1, 5 * D)):
            for wv in range(2):
                for half in range(2):
                    pb = pbc.next()
                    self.mm(pb[:, :], self.sel[:, wv, :], rows[:, c0 + half * 512:c0 + (half + 1) * 512], True, True,
                            [self.sel, rows], [pb])
                    self.cp(self.evac_eng(), self.G[a_][wv][:, half * 512:(half + 1) * 512], pb[:, :], [pb], [self.G[a_][wv]])
        ph.close()

    def norm_to_hT(self, ph, hT, A, Bcol0, tiles, tok0):
        S, P = self.S, self.P
        xr = ph.rot("sb", "xr", [128, D], F32, 3)
        junk = ph.sb("junk", [128, D], F32)
        ssr = ph.rot("sb", "ss", [128, 2], F32, 3)
        dgr = ph.rot("sb", "dg", [128, 128], F32, 2)
        ptr = ph.rot("ps", "pT", [128, 4, 128], F32, 3)
        cst = self.cst
        for i in tiles:
            wv = 1 if i < NCTX else 0
            xt = xr.next()
            self.ld(xt[:], S["xs"].ap[i * 128:(i + 1) * 128, :], S["xs"], xt)
            ss = ssr.next()
            self.memset("pool", ss[:], 0.0, [ss])
            self.act(junk[:], xt[:], AF.Square, [xt, ss], [junk, ss], accum_out=ss[:, 0:1])
            self.act(ss[:, 1:2], ss[:, 0:1], AF.Ln, [ss], [ss], scale=1.0 / D, bias=EPS)
            self.act(ss[:, 1:2], ss[:, 1:2], AF.Exp, [ss], [ss], scale=-0.5)
            dg = dgr.next()
            self.ts("dve", dg[:], cst[:, C_ID, :], ss[:, 1:2], None, ALU.mult, None, [cst, ss], [dg])
            for kg in range(2):
                pt = ptr.next()
                for kk in range(4):
                    k = kg * 4 + kk
                    self.mm(pt[:, kk, :], xt[:, k * 128:(k + 1) * 128], dg[:], True, True, [xt, dg], [pt])
                eng = self.evac_eng()
                for kk in range(4):
                    k = kg * 4 + kk
                    o = hT[:, k, i * 128 - tok0:(i + 1) * 128 - tok0]
                    if eng == "act":
                        self.act(o, pt[:, kk, :], AF.Identity, [pt, A, self.modT], [hT],
                                 scale=A[:, k, wv:wv + 1], bias=self.modT[:, Bcol0 + k, wv:wv + 1])
                    else:
                        self.ts("dve", o, pt[:, kk, :], A[:, k, wv:wv + 1], self.modT[:, Bcol0 + k, wv:wv + 1],
                                ALU.mult, ALU.add, [pt, A, self.modT], [hT])

    def ph_n1(self, l):
        self.hph = self.phase()
        self.hT = self.hph.sb("hT", [128, 8, T], BF16)
        ph = self.phase()
        self.norm_to_hT(ph, self.hT, self.A1, 0, range(NT), 0)
        ph.close()

    def load_w_bf(self, stg_rot, wb_rot, w_ap, c0, n, kchunks=8):
        st = stg_rot.next()
        wb = wb_rot.next()
        self.ld(st[:, 0:kchunks, 0:n], self.wview(w_ap, c0, n), None, st)
        self.cp("pool", wb[:, 0:kchunks, 0:n], st[:, 0:kchunks, 0:n], [st], [wb])
        return wb

    def tok_groups(self):
        return [(0, 512), (512, 512), (1024, 512), (1536, 512), (2048, 256)]

    def ph_proj(self, l):
        I, S, P = self.I, self.S, self.P
        hT = self.hT
        ph = self.phase()
        stg = ph.rot("sb", "wstg", [128, 8, 512], F32, 2)
        wbr = ph.rot("sb", "wbf", [128, 8, 512], BF16, 2)
        psr = ph.rot("ps", "pp", [128, 512], F32, 4)
        ptr = ph.rot("ps", "pt", [128, 4, 128], F32, 2)
        w_in = I["w_in"][l]
        cstb = self.cstb
        cst = self.cst

        scw = ph.sb("scw", [128, 10, 5], F32)
        scb = ph.sb("scb", [128, 10], F32)
        mcw = ph.sb("mcw", [128, 8, 5], F32)
        mcb = ph.sb("mcb", [128, 8], F32)
        self.ld(scw[:], I["sconvT"][l], None, scw)
        self.ld(scb[:], I["sconvb"][l], None, scb)
        self.ld(mcw[:], I["mconvT"][l], None, mcw)
        self.ld(mcb[:], I["mconvb"][l], None, mcb)
        upr = ph.rot("sb", "up", [128, T + 8], BF16, 2)
        for ub in upr.bufs:
            self.memset("dve", ub[:], 0.0, [ub])
        dgwr = ph.rot("sb", "dgw", [128, 5, 128], BF16, 2)
        sar = ph.rot("sb", "sact", [128, T], BF16, 2)
        tmo = ph.rot("sb", "tmo", [128, NT, 512], BF16, 1)
        cgroups = [(0, 256), (256, 512), (768, 512), (1280, 512), (1792, 512)]

        def pos(t):
            return t + 2 if t < 256 else t + 6

        conv_groups = [
            (0, 512, scw, scb, 0, None, (S["sx_tm"], 0)),
            (512, 512, scw, scb, 4, None, (S["sx_tm"], 512)),
            (2048, 256, scw, scb, 8, (S["sbc_fm"], 0), (S["sb_tm"], 0)),
            (2336, 512, mcw, mcb, 0, (S["mq_fm"], 0), None),
            (2848, 512, mcw, mcb, 4, (S["mk_fm"], 0), (S["mk_tm"], 0)),
        ]
        tasks = []

        def conv_task(wb, c0, ncols, cw, cb, ct0, fm, tm):
            tmb = tmo.next() if tm is not None else None
            ntile = ncols // 128
            for jj in range(ntile):
                ct = ct0 + jj
                up = upr.next()
                dgw = dgwr.next()
                for j in range(5):
                    self.ts("dve", dgw[:, j, :], cst[:, C_ID, :], cw[:, ct, j:j + 1], None, ALU.mult, None, [cst, cw], [dgw])
                for (t0, tn) in cgroups:
                    pp = psr.next()
                    for k in range(8):
                        self.mm(pp[:, 0:tn], wb[:, k, jj * 128:(jj + 1) * 128], hT[:, k, t0:t0 + tn], k == 0, k == 7, [wb, hT], [pp])
                    self.cp(self.evac_eng(), up[:, pos(t0):pos(t0) + tn], pp[:, 0:tn], [pp], [up])
                sa = sar.next()
                for (t0, tn) in cgroups:
                    pp = psr.next()
                    for j in range(5):
                        p0 = pos(t0) + j - 2
                        self.mm(pp[:, 0:tn], dgw[:, j, :], up[:, p0:p0 + tn], j == 0, j == 4, [dgw, up], [pp])
                    self.act(sa[:, t0:t0 + tn], pp[:, 0:tn], AF.Silu, [pp, cb], [sa], bias=cb[:, ct:ct + 1])
                is_c_tile = (c0 == 2048 and jj == 1)
                if fm is not None:
                    fb, r0 = fm
                    self.stq(fb.ap[r0 + jj * 128:r0 + (jj + 1) * 128, :], sa[:], sa, fb)
                if tm is not None and not is_c_tile:
                    for ig in range(0, NT, 4):
                        n_i = min(4, NT - ig)
                        pt = ptr.next()
                        for ii in range(n_i):
                            i = ig + ii
                            self.mm(pt[:, ii, :], sa[:, i * 128:(i + 1) * 128], cstb[:, C_ID, :], True, True, [sa, cstb], [pt])
                        self.cp(self.evac_eng(), tmb[:, ig:ig + n_i, jj * 128:(jj + 1) * 128], pt[:, 0:n_i, :], [pt], [tmb])
            if tm is not None:
                tb, tc0 = tm
                ncs = 128 if c0 == 2048 else ncols
                self.stq(tb.ap.rearrange("(i p) c -> p i c", p=128)[:, :, tc0:tc0 + ncs], tmb[:, :, 0:ncs], tmb, tb)


        for cg in conv_groups:
            tasks.append((cg[0], cg[1], (lambda wb, cg=cg: conv_task(wb, *cg))))

        gfm = ph.rot("sb", "gfm", [128, T], BF16, 2)

        def gla_task(wb, dst):
            for jj in range(4):
                gb = gfm.next()
                self.gla_fm_tile(psr, wb[:, :, jj * 128:(jj + 1) * 128], 128, hT, gb, wb)
                self.stq(dst.ap[jj * 128:(jj + 1) * 128, :], gb[:], gb, dst)

        def ga_task(wb):
            gb = gfm.next()
            self.gla_fm_tile(psr, wb[:, :, 0:32], 32, hT, gb, wb)
            self.stq(S["ga_fm"].ap[:, :], gb[0:32, :], gb, S["ga_fm"])

        for (c0, dst) in ((5424, S["gq_fm"]), (5936, S["gk_fm"])):
            tasks.append((c0, 512, (lambda wb, dst=dst: gla_task(wb, dst))))
        tasks.append((8496, 32, ga_task))

        tmg = []
        for h in range(2):
            tmg.append((1024 + h * 512, 512, S["sz_tm"], h * 512, AF.Silu, BF16))
        for h in range(2):
            tmg.append((7472 + h * 512, 512, S["gg_tm"], h * 512, AF.Silu, BF16))
        for h in range(2):
            tmg.append((4384 + h * 512, 512, S["mo_tm"], h * 512, AF.Sigmoid, BF16))
        for h in range(6):
            tmg.append((8528 + h * 512, 512, S["gates"], h * 512, AF.Sigmoid, BF16))
        for h in range(2):
            tmg.append((3360 + h * 512, 512, S["mv_tm"], h * 512, AF.Copy, BF16))
        tmg.append((5936, 512, S["gk_tm"], 0, AF.Copy, BF16))
        for h in range(2):
            tmg.append((6448 + h * 512, 512, S["gv_tm"], h * 512, AF.Copy, BF16))
        tmg.append((2304, 32, S["sdt"], 0, AF.Copy, F32))
        tmg.append((5408, 16, S["mg"], 0, AF.Copy, F32))
        stb = ph.rot("sb", "stb", [128, 512], BF16, 4)
        stf = ph.rot("sb", "stf", [128, 32], F32, 3)
        def tm_task(wb, c0, ncols, dst, dc0, func, dt):
            for i in range(NT):
                pp = psr.next()
                for k in range(8):
                    self.mm(pp[:, 0:ncols], hT[:, k, i * 128:(i + 1) * 128], wb[:, k, 0:ncols], k == 0, k == 7, [hT, wb], [pp])
                so = stb.next() if dt == BF16 else stf.next()
                if func == AF.Copy:
                    self.cp(self.evac_eng(), so[:, 0:ncols], pp[:, 0:ncols], [pp], [so])
                else:
                    self.act(so[:, 0:ncols], pp[:, 0:ncols], func, [pp], [so])
                self.stq(dst.ap[i * 128:(i + 1) * 128, dc0:dc0 + ncols], so[:, 0:ncols], so, dst)

        for tg in tmg:
            tasks.append((tg[0], tg[1], (lambda wb, tg=tg: tm_task(wb, *tg))))
        nxt = self.load_w_bf(stg, wbr, w_in, tasks[0][0], tasks[0][1])
        for ti, (c0, ncols, fn) in enumerate(tasks):
            wb = nxt
            if ti + 1 < len(tasks):
                nxt = self.load_w_bf(stg, wbr, w_in, tasks[ti + 1][0], tasks[ti + 1][1])
            fn(wb)
        ph.close()
        self.hph.close()

    def gla_fm_tile(self, psr, lhsT, m, hT, gb, wbuf):
        pp = psr.next()
        for k in range(8):
            self.mm(pp[0:m, 0:256], lhsT[:, k, :], hT[:, k, 0:256], k == 0, k == 7, [hT, wbuf], [pp])
        self.cp(self.evac_eng(), gb[0:m, 0:256], pp[0:m, 0:256], [pp], [gb])
        for q in range(4):
            pp = psr.next()
            t0 = 256 + q * 512
            for k in range(8):
                self.mm(pp[0:m, :], lhsT[:, k, :], hT[:, k, t0:t0 + 512], k == 0, k == 7, [hT, wbuf], [pp])
            dst = gb[0:m, 256:T].rearrange("p (c r) -> p r c", r=32)[:, q * 8:(q + 1) * 8, :]
            self.cp(self.evac_eng(), dst, pp[0:m, :].rearrange("p (r c) -> p r c", c=64), [pp], [gb])

    def chunk_order(self, d):
        if d == 0:
            return list(range(NT))
        return [1, 0] + list(range(NT - 1, 1, -1))

    def row_bc(self, ph, name, dram_row_ap, n):
        t = ph.sb(name, [128, n], F32)
        self.ld(t[:], dram_row_ap.partition_broadcast(128), None, t)
        return t

    def ph_ssd(self, l):
        I, S, P = self.I, self.S, self.P
        self.sub = self.sub0 if getattr(self, "ssd_calls", 0) == 0 else self.sub1
        ph = self.phase()
        cst, cstb = self.cst, self.cstb
        BT = ph.sb("BT", [64, 2, T], BF16)
        CT = ph.sb("CT", [64, 2, T], BF16)
        self.ld(BT[:], S["sbc_fm"].ap[0:128, :].rearrange("(g n) t -> n g t", n=64), S["sbc_fm"], BT)
        self.ld(CT[:], S["sbc_fm"].ap[128:256, :].rearrange("(g n) t -> n g t", n=64), S["sbc_fm"], CT)
        dtb = self.row_bc(ph, "dtb", I["sdtb"][l], 32)
        negA = self.row_bc(ph, "negA", I["salog"][l], 32)
        self.act(negA[:], negA[:], AF.Exp, [negA], [negA])
        self.ts("dve", negA[:], negA[:], -1.0, None, ALU.mult, None, [negA], [negA])
        Sts = [ph.sb("St%d" % d_, [64, 16, 64], F32) for d_ in range(2)]
        Sbs = [ph.sb("Sb%d" % d_, [64, 16, 64], BF16) for d_ in range(2)]
        smr = ph.rot("sb", "sm", [128, 6, 16], F32, 4)
        rhsA = ph.rot("sb", "rhsA", [128, 16, 128], F32, 3)
        Dm = ph.rot("sb", "Dm", [128, 16, 128], BF16, 4)
        CBm = ph.rot("sb", "CBm", [128, 2, 128], BF16, 4)
        sTr = ph.rot("sb", "sT", [128, 16, 128], BF16, 4)
        xdtr = ph.rot("sb", "xdt", [128, 16, 64], BF16, 4)
        xhr = ph.rot("sb", "xh", [128, 16, 64], BF16, 4)
        yintr = ph.rot("sb", "yint", [128, 16, 64], F32, 4)
        yor = ph.rot("sb", "yo", [128, 16, 64], BF16, 4)
        t1r = ph.rot("sb", "t1", [128, 16, 64], F32, 4)
        sttmp = ph.rot("sb", "sttmp", [128, 512], F32, 4)
        pseg2 = [ph.ps("pseg%d" % i_, [128, 4, 128]) for i_ in range(2)]
        pcb = ph.ps("pcb", [128, 512])
        pyi2 = [ph.ps("pyi%d" % i_, [128, 8, 64]) for i_ in range(2)]
        pyn2 = [ph.ps("pyn%d" % i_, [128, 8, 64]) for i_ in range(2)]
        pstu = ph.ps("pstu", [128, 512])
        raw = ph.sb("dtraw", [128, NT, 32], F32)
        pre = ph.sb("pre", [128, NT, 2, 2, 16], F32)
        self.ld(raw[:], S["sdt"].ap.rearrange("(i p) c -> p i c", p=128), S["sdt"], raw)
        for d_ in range(2):
            hs_ = slice(d_ * 16, (d_ + 1) * 16)
            self.tt("dve", pre[:, :, d_, 0, :], raw[:, :, hs_], dtb[:, hs_].unsqueeze(1).to_broadcast([128, NT, 16]), ALU.add,
                    [raw, dtb], [pre])
            self.act(pre[:, :, d_, 0, :], pre[:, :, d_, 0, :], AF.Exp, [pre], [pre])
            self.act(pre[:, :, d_, 0, :], pre[:, :, d_, 0, :], AF.Ln, [pre], [pre], bias=1.0)
            self.tt("dve", pre[:, :, d_, 1, :], pre[:, :, d_, 0, :], negA[:, hs_].unsqueeze(1).to_broadcast([128, NT, 16]), ALU.mult,
                    [pre, negA], [pre])

        def run_dir(d):
            xcr = ph.rot("sb", "xc%d" % d, [128, 16, 64], BF16, 3)
            bcr = ph.rot("sb", "bc%d" % d, [128, 128], BF16, 3)
            St = Sts[d]; Sb = Sbs[d]
            M = C_U if d == 0 else C_L
            Ms = C_LS if d == 0 else C_US
            self.memset("dve", St[:], 0.0, [St])
            self.memset("dve", Sb[:], 0.0, [Sb])
            ydst = S["ys_f"] if d == 0 else S["ys_b"]
            self.ssd_calls = getattr(self, "ssd_calls", 0) + (1 if d == 0 else 0)
            order = self.chunk_order(d)[:(self.nch if self.ssd_calls == 1 else self.nch1)]
            loaded = {}

            def issue(c):
                xc = xcr.next(); bc = bcr.next()
                self.ld(xc[:].rearrange("p h q -> p (h q)"), S["sx_tm"].ap[c * 128:(c + 1) * 128, :], S["sx_tm"], xc)
                self.ld(bc[:], S["sb_tm"].ap[c * 128:(c + 1) * 128, :], S["sb_tm"], bc)
                loaded[c] = (xc, bc)

            issue(order[0])
            for ci, c in enumerate(order):
                if ci + 1 < len(order):
                    issue(order[ci + 1])
                xc, bc = loaded.pop(c)
                tok = slice(c * 128, (c + 1) * 128)
                sm = smr.next()
                self.cp("dve", sm[:, 0:2, :], pre[:, c, d, :, :], [pre], [sm])
                if FINE_IL and (FINE_MASK >> 0) & 1:
                    yield
                xdt = xdtr.next()
                self.tt("dve", xdt[:], xc[:], sm[:, 0, :].unsqueeze(2).to_broadcast([128, 16, 64]), ALU.mult, [xc, sm], [xdt])
                ra = rhsA.next()
                self.tt("dve", ra[:], cst[:, M, :].unsqueeze(1).to_broadcast([128, 16, 128]),
                        sm[:, 1, :].unsqueeze(2).to_broadcast([128, 16, 128]), ALU.mult, [cst, sm], [ra])
                self.mm(pcb[:, 0:16], cst[:, M, :], sm[:, 1, :], True, True, [cst, sm], [pcb])
                self.mm(pcb[:, 16:32], cst[:, Ms, :], sm[:, 1, :], True, True, [cst, sm], [pcb])
                self.mm(pcb[:, 32:48], cst[:, C_ONE, :], sm[:, 1, :], True, True, [cst, sm], [pcb])
                for g in range(2):
                    self.mm(pcb[:, 256 + g * 128:256 + (g + 1) * 128], BT[:, g, tok], CT[:, g, tok], True, True, [BT, CT], [pcb])
                self.act(sm[:, 2:5, :], pcb[:, 0:48].rearrange("p (a h) -> p a h", h=16), AF.Exp, [pcb], [sm])
                cbm = CBm.next()
                self.tt("dve", cbm[:], pcb[:, 256:512].rearrange("p (g t) -> p g t", g=2),
                        cstb[:, M, :].unsqueeze(1).to_broadcast([128, 2, 128]), ALU.mult, [pcb, cstb], [cbm])
                if FINE_IL and (FINE_MASK >> 1) & 1:
                    yield
                dm = Dm.next()
                for g in range(2):
                    for q in range(2):
                        h0 = g * 8 + q * 4
                        self.mm(pseg2[q][:], cst[:, Ms, :], ra[:, h0:h0 + 4, :], True, True, [cst, ra], [pseg2[q]])
                    for q in range(2):
                        h0 = g * 8 + q * 4
                        self.act(dm[:, h0:h0 + 4, :], pseg2[q][:], AF.Exp, [pseg2[q]], [dm])
                if FINE_IL and (FINE_MASK >> 2) & 1:
                    yield
                sT = sTr.next()
                for g in range(2):
                    self.tt("dve" if g == 0 else "pool", sT[:, g * 8:(g + 1) * 8, :], dm[:, g * 8:(g + 1) * 8, :],
                            cbm[:, g:g + 1, :].to_broadcast([128, 8, 128]), ALU.mult, [dm, cbm], [sT])
                yint = yintr.next()
                for hb in range(2):
                    for h in range(hb * 8, (hb + 1) * 8):
                        self.mm(pyi2[hb][:, h - hb * 8, :], sT[:, h, :], xdt[:, h, :], True, True, [sT, xdt], [pyi2[hb]])
                    self.cp("act", yint[:, hb * 8:(hb + 1) * 8, :], pyi2[hb][:], [pyi2[hb]], [yint])
                if FINE_IL and (FINE_MASK >> 3) & 1:
                    yield
                t1 = t1r.next()
                for g in range(2):
                    self.mm(pyn2[g][:], CT[:, g, tok], Sb[:, g * 8:(g + 1) * 8, :], True, True, [CT, Sb], [pyn2[g]])
                    self.tt("dve", t1[:, g * 8:(g + 1) * 8, :], pyn2[g][:], sm[:, 2, g * 8:(g + 1) * 8].unsqueeze(2).to_broadcast([128, 8, 64]),
                            ALU.mult, [pyn2[g], sm], [t1])
                yo = yor.next()
                self.tt("dve", yo[:], t1[:], yint[:], ALU.add, [t1, yint], [yo])
                self.stq(ydst.ap[tok, :], yo[:].rearrange("p h q -> p (h q)"), yo, ydst)
                if FINE_IL and (FINE_MASK >> 4) & 1:
                    yield
                xh = xhr.next()
                self.tt("dve", xh[:], xdt[:], sm[:, 3, :].unsqueeze(2).to_broadcast([128, 16, 64]), ALU.mult, [xdt, sm], [xh])
                if FINE_IL and (FINE_MASK >> 5) & 1:
                    yield
                for g in range(2):
                    hs = slice(g * 8, (g + 1) * 8)
                    pstu = pseg2[g]
                    pstv = pstu[:].rearrange("p a b -> p (a b)")
                    self.mm(pstv[0:64, :], bc[:, g * 64:(g + 1) * 64], xh[:, hs, :], True, True, [bc, xh], [pstu])
                    if os.environ.get("ST_ALT", "0") == "1":
                        stt_ = sttmp.next()
                        self.act(stt_[0:64, :], pstu[0:64, :], AF.Copy, [pstu], [stt_])
                        self.tt("pool", St[:, hs, :], St[:, hs, :], sm[0:64, 4, hs].unsqueeze(2).to_broadcast([64, 8, 64]),
                                ALU.mult, [St, sm], [St])
                        self.tt("pool", St[:, hs, :], St[:, hs, :], stt_[0:64, :].rearrange("p (h q) -> p h q", q=64), ALU.add,
                                [St, stt_], [St])
                    else:
                        self.tt("dve", St[:, hs, :], St[:, hs, :], sm[0:64, 4, hs].unsqueeze(2).to_broadcast([64, 8, 64]),
                                ALU.mult, [St, sm], [St])
                        self.tt("dve", St[:, hs, :], St[:, hs, :], pstv[0:64, :].rearrange("p (h q) -> p h q", q=64), ALU.add,
                                [St, pstu], [St])
                if FINE_IL and (FINE_MASK >> 6) & 1:
                    yield
                self.cp("act", Sb[:], St[:], [St], [Sb])
                yield
        self.interleave([run_dir(0), run_dir(1)])
        ph.close()

    def ph_ml(self, l):
        I, S, P = self.I, self.S, self.P
        ph = self.phase()
        cst, cstb = self.cst, self.cstb
        qT = ph.sb("qT", [128, 4, T], BF16)
        kT = ph.sb("kT", [128, 4, T], BF16)
        self.ld(qT[:], S["mq_fm"].ap.rearrange("(h d) t -> d h t", d=128), S["mq_fm"], qT)
        self.ld(kT[:], S["mk_fm"].ap.rearrange("(h d) t -> d h t", d=128), S["mk_fm"], kT)
        bi = self.row_bc(ph, "bi", I["mib"][l], 8)
        bfb = self.row_bc(ph, "bfb", I["mfb"][l], 8)
        mneg = ph.sb("mneg", [128, 2, 4, 128], F32)
        for d in range(2):
            M = C_U if d == 0 else C_L
            self.ts("dve", mneg[:, d, :, :], cst[:, M, :].unsqueeze(1).to_broadcast([128, 4, 128]), 30000.0, -30000.0,
                    ALU.mult, ALU.add, [cst], [mneg])
        Css = [ph.sb("Cs%d" % d_, [128, 4, 257], F32) for d_ in range(2)]
        Cbs = [ph.sb("Cb%d" % d_, [128, 4, 257], BF16) for d_ in range(2)]
        smr = ph.rot("sb", "sm", [128, 8, 4], F32, 4)
        rhsA = ph.rot("sb", "rhsA", [128, 4, 128], F32, 3)
        libc = ph.rot("sb", "libc", [128, 4, 128], F32, 3)
        Dm = ph.rot("sb", "Dm", [128, 4, 128], F32, 3)
        Eb = ph.rot("sb", "Eb", [128, 4, 128], F32, 3)
        qtr = ph.rot("sb", "qt", [128, 4, 128], BF16, 4)
        sTr = ph.rot("sb", "sT", [128, 4, 128], BF16, 4)
        khr = ph.rot("sb", "kh", [128, 4, 128], BF16, 4)
        yor = ph.rot("sb", "yo", [128, 4, 256], BF16, 3)
        pseg = ph.ps("pseg", [128, 4, 128])
        pE = ph.ps("pE", [128, 4, 128])
        psc = ph.ps("psc", [128, 4, 128])
        pnum = [ph.ps("pnum%d" % h, [128, 512]) for h in range(4)]
        pst = ph.ps("pst", [128, 512])
        graw = ph.sb("graw", [128, NT, 16], F32)
        prem = ph.sb("prem", [128, NT, 2, 2, 4], F32)
        self.ld(graw[:], S["mg"].ap.rearrange("(i p) c -> p i c", p=128), S["mg"], graw)
        for d_ in range(2):
            hs_ = slice(d_ * 4, (d_ + 1) * 4)
            self.stt("dve", prem[:, :, d_, 0, :], graw[:, :, hs_], LN_ISQ, bi[:, hs_].unsqueeze(1).to_broadcast([128, NT, 4]),
                     ALU.add, ALU.add, [graw, bi], [prem])
            self.tt("dve", prem[:, :, d_, 1, :], graw[:, :, 8 + d_ * 4:8 + (d_ + 1) * 4],
                    bfb[:, hs_].unsqueeze(1).to_broadcast([128, NT, 4]), ALU.add, [graw, bfb], [prem])
            self.act(prem[:, :, d_, 1, :], prem[:, :, d_, 1, :], AF.Exp, [prem], [prem], scale=-1.0)
            self.act(prem[:, :, d_, 1, :], prem[:, :, d_, 1, :], AF.Ln, [prem], [prem], bias=1.0)

        def run_dir(d):
            kcr = ph.rot("sb", "kc%d" % d, [128, 4, 128], BF16, 3)
            vcr = ph.rot("sb", "vc%d" % d, [128, 4, 257], BF16, 3)
            for vb in vcr.bufs:
                self.memset("dve", vb[:], 1.0, [vb])
            Cs = Css[d]; Cb = Cbs[d]
            M = C_U if d == 0 else C_L
            Ms = C_LS if d == 0 else C_US
            self.memset("dve", Cs[:], 0.0, [Cs])
            self.memset("dve", Cb[:], 0.0, [Cb])
            ydst = S["ym_f"] if d == 0 else S["ym_b"]
            order = self.chunk_order(d)
            loaded = {}

            def issue(c):
                kc = kcr.next(); vc = vcr.next()
                self.ld(kc[:].rearrange("p h q -> p (h q)"), S["mk_tm"].ap[c * 128:(c + 1) * 128, :], S["mk_tm"], kc)
                self.ld(vc[:, :, 0:256], S["mv_tm"].ap[c * 128:(c + 1) * 128, :].rearrange("p (h q) -> p h q", q=256), S["mv_tm"], vc)
                loaded[c] = (kc, vc)

            issue(order[0])
            for ci, c in enumerate(order):
                if ci + 1 < len(order):
                    issue(order[ci + 1])
                kc, vc = loaded.pop(c)
                tok = slice(c * 128, (c + 1) * 128)
                sm = smr.next()
                self.cp("dve", sm[:, 0:2, :], prem[:, c, d, :, :], [prem], [sm])
                ra = rhsA.next()
                self.stt("dve", ra[:], cst[:, M, :].unsqueeze(1).to_broadcast([128, 4, 128]), -1.0,
                         sm[:, 1, :].unsqueeze(2).to_broadcast([128, 4, 128]), ALU.mult, ALU.mult, [cst, sm], [ra])
                lb = libc.next()
                self.tt("pool", lb[:], mneg[:, d, :, :], sm[:, 0, :].unsqueeze(2).to_broadcast([128, 4, 128]), ALU.add, [mneg, sm], [lb])
                if FINE_IL:
                    yield
                self.mm(pseg[:].rearrange("p h t -> p (h t)"), cst[:, Ms, :], ra[:].rearrange("p h t -> p (h t)"), True, False, [cst, ra], [pseg])
                self.mm(pseg[:].rearrange("p h t -> p (h t)"), cst[:, C_ID, :], lb[:].rearrange("p h t -> p (h t)"), False, True, [cst, lb], [pseg])
                self.mm(pE[:].rearrange("p h t -> p (h t)"), cst[:, C_ONE, :], ra[:].rearrange("p h t -> p (h t)"), True, True, [cst, ra], [pE])
                dm = Dm.next()
                self.act(dm[:], pseg[:], AF.Exp, [pseg], [dm])
                eb = Eb.next()
                self.act(eb[:], pE[:], AF.Exp, [pE], [eb])
                if FINE_IL:
                    yield
                qt = qtr.next()
                self.tt("pool", qt[:], qT[:, :, tok], eb[:], ALU.mult, [qT, eb], [qt])
                for h in range(4):
                    self.mm(psc[:, h, :], kT[:, h, tok], qT[:, h, tok], True, True, [kT, qT], [psc])
                sT = sTr.next()
                self.tt("dve", sT[:], psc[:], dm[:], ALU.mult, [psc, dm], [sT])
                if FINE_IL:
                    yield
                for h in range(4):
                    self.mm(pnum[h][:, 0:257], sT[:, h, :], vc[:, h, :], True, False, [sT, vc], [pnum[h]])
                    self.mm(pnum[h][:, 0:257], qt[:, h, :], Cb[:, h, :], False, True, [qt, Cb], [pnum[h]])
                for h in range(4):
                    self.cp("act" if h % 2 == 0 else "dve", sm[:, 4, h:h + 1], pnum[h][:, 256:257], [pnum[h]], [sm])
                self.stt("dve", sm[:, 5, :], sm[:, 4, :], -1.0, sm[:, 4, :], ALU.mult, ALU.max, [sm], [sm])
                self.ts("dve", sm[:, 5, :], sm[:, 5, :], 1.0, None, ALU.max, None, [sm], [sm])
                self.P.op("dve", lambda e, o=sm[:, 6, :], i_=sm[:, 5, :]: e.reciprocal(out=o, in_=i_), [sm], [sm])
                yo = yor.next()
                for h in range(4):
                    if h % 2 == 0:
                        self.act(yo[:, h, :], pnum[h][:, 0:256], AF.Copy, [pnum[h], sm], [yo], scale=sm[:, 6, h:h + 1])
                    else:
                        self.ts("dve", yo[:, h, :], pnum[h][:, 0:256], sm[:, 6, h:h + 1], None, ALU.mult, None, [pnum[h], sm], [yo])
                self.stq(ydst.ap[tok, :], yo[:].rearrange("p h q -> p (h q)"), yo, ydst)
                if FINE_IL:
                    yield
                self.mm(pst[:, 0:4], cst[:, Ms, :], sm[:, 1, :], True, True, [cst, sm], [pst])
                self.mm(pst[:, 4:8], cst[:, C_ONE, :], sm[:, 1, :], True, True, [cst, sm], [pst])
                self.tt("dve", sm[:, 2, :], sm[:, 0, :], pst[:, 0:4], ALU.subtract, [sm, pst], [sm])
                self.act(sm[:, 2, :], sm[:, 2, :], AF.Exp, [sm], [sm])
                self.act(sm[:, 3, :], pst[:, 4:8], AF.Exp, [pst], [sm], scale=-1.0)
                kh = khr.next()
                self.tt("pool", kh[:], kc[:], sm[:, 2, :].unsqueeze(2).to_broadcast([128, 4, 128]), ALU.mult, [kc, sm], [kh])
                for h in range(4):
                    pb_ = pst if h % 2 == 0 else pE
                    pv_ = pst[:, 0:257] if h % 2 == 0 else pE[:].rearrange("p h t -> p (h t)")[:, 0:257]
                    self.mm(pv_, kh[:, h, :], vc[:, h, :], True, True, [kh, vc], [pb_])
                    self.stt("dve", Cs[:, h, :], Cs[:, h, :], sm[:, 3, h:h + 1], pv_, ALU.mult, ALU.add, [Cs, sm, pb_], [Cs])
                self.cp("act", Cb[:], Cs[:], [Cs], [Cb])
                yield
        self.interleave([run_dir(0), run_dir(1)])
        ph.close()

    def gla_rows(self, dram_ap, c):
        if c < NCTX:
            return [(slice(0, 128), dram_ap[c * 128:(c + 1) * 128, :])]
        col0 = (c - NCTX) * 4
        lat = dram_ap[256:T, :].rearrange("(r c) f -> c r f", c=64)
        return [(slice(j * 32, (j + 1) * 32), lat[col0 + j]) for j in range(4)]

    def ph_gla(self, l):
        I, S, P = self.I, self.S, self.P
        self.wph = self.phase()
        self.Wb = [self.wph.sb(n, [128, 8, D], BF16) for n in ("w_b_ssd", "w_b_ml", "w_b_gla")]
        ph = self.phase()
        bstg = ph.rot("sb", "bstg", [128, 8, 512], F32, 1)

        def bg():
            for wb_, n in zip(self.Wb, ("w_b_ssd", "w_b_ml", "w_b_gla")):
                for c0 in range(0, D, 512):
                    st = bstg.next()
                    self.ld(st[:], I[n][l].rearrange("(k p) n -> p k n", p=128)[:, :, c0:c0 + 512], None, st)
                    for _ in range(6):
                        yield
                    self.cp("pool", wb_[:, :, c0:c0 + 512], st[:], [st], [wb_])
                    for _ in range(3):
                        yield
        cst, cstb = self.cst, self.cstb
        qT = ph.sb("qT", [128, 4, T], BF16)
        kT = ph.sb("kT", [128, 4, T], BF16)
        self.ld(qT[:], S["gq_fm"].ap.rearrange("(h d) t -> d h t", d=128), S["gq_fm"], qT)
        self.ld(kT[:], S["gk_fm"].ap.rearrange("(h d) t -> d h t", d=128), S["gk_fm"], kT)
        aT = [ph.sb("aT%d" % d, [17, T], BF16) for d in range(2)]
        for d in range(2):
            self.memset("dve", aT[d][:], 1.0, [aT[d]])
            self.ld(aT[d][0:16, :], S["ga_fm"].ap[d * 16:(d + 1) * 16, :], S["ga_fm"], aT[d])
        aupf = ph.sb("aupf", [17, 2, 512], F32)
        aupb = ph.sb("aupb", [17, 2, 512], BF16)
        self.ld(aupf[:], I["aup"][l].rearrange("d r n -> r d n"), None, aupf)
        self.cp("dve", aupb[:], aupf[:], [aupf], [aupb])
        Sss = [ph.sb("Ss%d" % d_, [128, 4, 256], F32) for d_ in range(2)]
        Sbfs = [ph.sb("Sbf%d" % d_, [128, 4, 256], BF16) for d_ in range(2)]
        nlgr = ph.rot("sb", "nlg", [128, 512], F32, 2)
        eqr = ph.rot("sb", "eq", [128, 4, 128], F32, 2)
        ekr = ph.rot("sb", "ek", [128, 4, 128], F32, 2)
        qtr = ph.rot("sb", "qt", [128, 4, 128], BF16, 3)
        ktr = ph.rot("sb", "kt", [128, 4, 128], BF16, 3)
        sTr = ph.rot("sb", "sT", [128, 4, 128], BF16, 3)
        wkr = ph.rot("sb", "wk", [128, 512], F32, 2)
        khr = ph.rot("sb", "kh", [128, 512], BF16, 3)
        elr = ph.rot("sb", "el", [128, 4], F32, 4)
        yor = ph.rot("sb", "yo", [128, 4, 256], BF16, 3)
        plg = ph.ps("plg", [128, 512])
        pbT = ph.ps("pbT", [128, 4, 128])
        psc = ph.ps("psc", [128, 4, 128])
        po2 = [ph.ps("po%d" % i_, [128, 2, 256]) for i_ in range(2)]
        pdl = ph.ps("pdl", [128, 512])
        pstp2 = [ph.ps("pstp%d" % i_, [128, 2, 256]) for i_ in range(2)]
        def run_dir(d):
            kcr = ph.rot("sb", "kc%d" % d, [128, 512], BF16, 3)
            vcr = ph.rot("sb", "vc%d" % d, [128, 4, 256], BF16, 3)
            Ss = Sss[d]; Sbf = Sbfs[d]
            M = C_U if d == 0 else C_L
            Ms = C_LS if d == 0 else C_US
            last = 127 if d == 0 else 0
            self.memset("dve", Ss[:], 0.0, [Ss])
            self.memset("dve", Sbf[:], 0.0, [Sbf])
            ydst = S["yg_f"] if d == 0 else S["yg_b"]
            order = self.chunk_order(d)
            loaded = {}

            def issue(c):
                kc = kcr.next(); vc = vcr.next()
                for (ps_, ap) in self.gla_rows(S["gk_tm"].ap, c):
                    self.ld(kc[ps_, :], ap, S["gk_tm"], kc)
                for (ps_, ap) in self.gla_rows(S["gv_tm"].ap, c):
                    self.ld(vc[ps_, :, :].rearrange("p h q -> p (h q)"), ap, S["gv_tm"], vc)
                loaded[c] = (kc, vc)

            issue(order[0])
            for ci, c in enumerate(order):
                if ci + 1 < len(order):
                    issue(order[ci + 1])
                kc, vc = loaded.pop(c)
                tok = slice(c * 128, (c + 1) * 128)
                self.mm(plg[:, :], aT[d][:, tok], aupb[:, d, :], True, True, [aT[d], aupb], [plg])
                nlg = nlgr.next()
                self.act(nlg[:], plg[:], AF.Exp, [plg], [nlg], scale=-1.0)
                self.act(nlg[:], nlg[:], AF.Ln, [nlg], [nlg], bias=1.0)
                if FINE_IL:
                    yield
                for h in range(4):
                    self.mm(pbT[:, h, :], nlg[:, h * 128:(h + 1) * 128], cst[:, M, :], True, True, [nlg, cst], [pbT])
                eq = eqr.next(); ek = ekr.next()
                self.act(eq[:], pbT[:], AF.Exp, [pbT], [eq], scale=-1.0 / 16.0)
                self.act(ek[:], pbT[:], AF.Exp, [pbT], [ek], scale=1.0 / 16.0)
                el = elr.next()
                self.cp("dve", el[:], eq[:, :, last], [eq], [el])
                if FINE_IL:
                    yield
                qt = qtr.next(); kt = ktr.next()
                self.stt("dve", qt[:], qT[:, :, tok], 128.0 ** -0.5, eq[:], ALU.mult, ALU.mult, [qT, eq], [qt])
                self.tt("pool", kt[:], kT[:, :, tok], ek[:], ALU.mult, [kT, ek], [kt])
                for h in range(4):
                    self.mm(psc[:, h, :], kt[:, h, :], qt[:, h, :], True, True, [kt, qt], [psc])
                sT = sTr.next()
                self.tt("dve", sT[:], psc[:], cst[:, M, :].unsqueeze(1).to_broadcast([128, 4, 128]), ALU.mult, [psc, cst], [sT])
                if FINE_IL:
                    yield
                for h in range(4):
                    pq = po2[h // 2]
                    self.mm(pq[:, h % 2, :], sT[:, h, :], vc[:, h, :], True, False, [sT, vc], [pq])
                    self.mm(pq[:, h % 2, :], qt[:, h, :], Sbf[:, h, :], False, True, [qt, Sbf], [pq])
                yo = yor.next()
                self.cp("act", yo[:, 0:2, :], po2[0][:], [po2[0]], [yo])
                self.cp("dve", yo[:, 2:4, :], po2[1][:], [po2[1]], [yo])
                for (ps_, ap) in self.gla_rows(ydst.ap, c):
                    self.stq(ap, yo[ps_, :, :].rearrange("p h q -> p (h q)"), yo, ydst)
                if FINE_IL:
                    yield
                self.mm(pdl[:, :], cst[:, Ms, :], nlg[:, :], True, True, [cst, nlg], [pdl])
                wk = wkr.next()
                self.act(wk[:], pdl[:], AF.Exp, [pdl], [wk], scale=-1.0 / 16.0)
                kh = khr.next()
                self.tt("pool", kh[:], kc[:], wk[:], ALU.mult, [kc, wk], [kh])
                for h in (0, 2, 1, 3):
                    pq = pstp2[h // 2]
                    self.mm(pq[:, h % 2, :], kh[:, h * 128:(h + 1) * 128], vc[:, h, :], True, True, [kh, vc], [pq])
                    self.stt("dve", Ss[:, h, :], Ss[:, h, :], el[:, h:h + 1], pq[:, h % 2, :], ALU.mult, ALU.add, [Ss, el, pq], [Ss])
                self.cp("act", Sbf[:], Ss[:], [Ss], [Sbf])
                yield
        self.interleave([run_dir(0), run_dir(1), bg()])
        ph.close()

    def load_resident_w(self, ph, stg, name, w_ap, kchunks, ncols):
        wb = ph.sb(name, [128, kchunks, ncols], BF16)
        for c0 in range(0, ncols, 512):
            for k0 in range(0, kchunks, 8):
                kn = min(8, kchunks - k0)
                st = stg.next()
                self.ld(st[:, 0:kn, :], w_ap.rearrange("(k p) n -> p k n", p=128)[:, k0:k0 + kn, c0:c0 + 512], None, st)
                self.cp("act", wb[:, k0:k0 + kn, c0:c0 + 512], st[:, 0:kn, :], [st], [wb])
        return wb

    def group_rstd(self, y, ng, gs, junk, ss):
        self.memset("pool", ss[:], 0.0, [ss])
        for g in range(ng):
            self.act(junk[:, 0:gs], y[:, g * gs:(g + 1) * gs], AF.Square, [y, ss], [junk, ss], accum_out=ss[:, g:g + 1])
        self.act(ss[:, ng:2 * ng], ss[:, 0:ng], AF.Ln, [ss], [ss], scale=1.0 / gs, bias=EPS)
        self.act(ss[:, ng:2 * ng], ss[:, ng:2 * ng], AF.Exp, [ss], [ss], scale=-0.5)

    def ph_epi(self, l):
        I, S, P = self.I, self.S, self.P
        last = (l == NL - 1)
        ph = self.phase()
        cst, cstb = self.cst, self.cstb
        wout = ph.sb("w_out", [128, 8, D], BF16)
        Wb = list(self.Wb) + [wout]
        inner = self.phase()
        stg = inner.rot("sb", "wstg", [128, 8, 512], F32, 2)
        for c0 in range(0, D, 512):
            st = stg.next()
            self.ld(st[:], I["w_out"][l].rearrange("(k p) n -> p k n", p=128)[:, :, c0:c0 + 512], None, st)
            self.cp("pool", wout[:, :, c0:c0 + 512], st[:], [st], [wout])
        inner.close()
        nw = []
        for n in ("snwT", "mnwT", "gnwT"):
            t = ph.sb(n, [128, 8], F32)
            self.ld(t[:], I[n][l], None, t)
            nw.append(t)
        dD = self.row_bc(ph, "dD", I["sd"][l], 16)
        yfr = ph.rot("sb", "yf", [128, D], BF16, 6)
        ybr = ph.rot("sb", "yb", [128, D], BF16, 6)
        a1r = ph.rot("sb", "a1", [128, D], BF16, 6)
        a2r = ph.rot("sb", "a2", [128, D], BF16, 2)
        gtr = ph.rot("sb", "gt", [128, 3 * D], BF16, 2)
        xtr = ph.rot("sb", "xt", [128, D], F32, 2)
        y1r = ph.rot("sb", "y1", [128, D], F32, 3)
        y2r = ph.rot("sb", "y2", [128, D], F32, 1)
        junkr = ph.rot("sb", "junk", [128, 512], BF16, 3)
        ssr = ph.rot("sb", "ss", [128, 8], F32, 4)
        ybf = ph.rot("sb", "ybf", [128, D], BF16, 7)
        yTr = ph.rot("sb", "yT", [128, 8, 128], BF16, 3)
        mrgr = ph.rot("sb", "mrg", [128, D], F32, 1)
        tmpr = ph.rot("sb", "tmp", [128, D], F32, 3)
        xn = ph.rot("sb", "xn", [128, D], F32, 1)
        ptr = ph.rot("ps", "pT", [128, 4, 128], F32, 2)
        ppr = ph.rot("ps", "pp", [128, 512], F32, 6)
        tiles = [i for i in range(NT) if not (last and i < NCTX)]
        BR = ((S["ys_f"], S["ys_b"], S["sx_tm"], 0, 2, 512),
              (S["ym_f"], S["ym_b"], S["mo_tm"], D, 4, 256),
              (S["yg_f"], S["yg_b"], S["gg_tm"], 2 * D, 4, 256))

        def issue_loads(i):
            rows = slice(i * 128, (i + 1) * 128)
            L = {}
            for bi_ in range(3):
                sf, sb_, sa = BR[bi_][0:3]
                yf = yfr.next(); yb = ybr.next(); a1 = a1r.next()
                self.ld(yf[:], sf.ap[rows, :], sf, yf)
                self.ld(yb[:], sb_.ap[rows, :], sb_, yb)
                self.ld(a1[:], sa.ap[rows, :], sa, a1)
                L[bi_] = (yf, yb, a1)
            L["a2"] = a2r.next(); self.ld(L["a2"][:], S["sz_tm"].ap[rows, :], S["sz_tm"], L["a2"])
            return L

        def issue_loads_b(i, L):
            rows = slice(i * 128, (i + 1) * 128)
            L["gt"] = gtr.next(); self.ld(L["gt"][:], S["gates"].ap[rows, :], S["gates"], L["gt"])
            L["xt"] = xtr.next(); self.ld(L["xt"][:], S["xs"].ap[rows, :], S["xs"], L["xt"])

        def transpose_proj_gen(src_bf, scale_t, W, out):
            yT = yTr.next()
            for kg in range(2):
                pt = ptr.next()
                for kk in range(4):
                    k = kg * 4 + kk
                    self.mm(pt[:, kk, :], src_bf[:, k * 128:(k + 1) * 128], cstb[:, C_ID, :], True, True, [src_bf, cstb], [pt])
                eng = self.evac_eng()
                for kk in range(4):
                    k = kg * 4 + kk
                    if scale_t is None:
                        self.cp(eng, yT[:, k, :], pt[:, kk, :], [pt], [yT])
                    else:
                        if eng == "act":
                            self.act(yT[:, k, :], pt[:, kk, :], AF.Copy, [pt, scale_t], [yT], scale=scale_t[:, k:k + 1])
                        else:
                            self.ts("dve", yT[:, k, :], pt[:, kk, :], scale_t[:, k:k + 1], None, ALU.mult, None, [pt, scale_t], [yT])
                yield
            pps = []
            for half in range(2):
                pp = ppr.next()
                for k in range(8):
                    self.mm(pp[:, :], yT[:, k, :], W[:, k, half * 512:(half + 1) * 512], k == 0, k == 7, [yT, W], [pp])
                pps.append(pp)
            out["pp"] = pps

        def prep(bi_, L, out):
            goff, ng, gs = BR[bi_][3:6]
            yf, yb, a1 = L[bi_]
            y1 = y1r.next()
            if bi_ == 0:
                a2 = L["a2"]; y2 = y2r.next()
            self.tt("pool", y1[:], yf[:], yb[:], ALU.add, [yf, yb], [y1])
            if bi_ == 0:
                self.tt("dve", y2[:].rearrange("p (h q) -> p h q", q=64), a1[:].rearrange("p (h q) -> p h q", q=64),
                        dD[:].unsqueeze(2).to_broadcast([128, 16, 64]), ALU.mult, [a1, dD], [y2])
                yield
                self.tt("pool", y1[:], y1[:], y2[:], ALU.add, [y1, y2], [y1])
                self.tt("dve", y1[:], y1[:], a2[:], ALU.mult, [y1, a2], [y1])
            yield
            ss = ssr.next()
            self.group_rstd(y1, ng, gs, junkr.next(), ss)
            yield
            yb16 = ybf.next()
            for g in range(ng):
                if bi_ == 0:
                    self.ts("dve", yb16[:, g * gs:(g + 1) * gs], y1[:, g * gs:(g + 1) * gs], ss[:, ng + g:ng + g + 1], None,
                            ALU.mult, None, [y1, ss], [yb16])
                else:
                    self.stt("dve", yb16[:, g * gs:(g + 1) * gs], y1[:, g * gs:(g + 1) * gs], ss[:, ng + g:ng + g + 1],
                             a1[:, g * gs:(g + 1) * gs], ALU.mult, ALU.mult, [y1, ss, a1], [yb16])
            out[bi_] = yb16
            yield

        def projb(bi_, yb16, gt, res):
            goff = BR[bi_][3]
            out = {}
            for _ in transpose_proj_gen(yb16, nw[bi_], Wb[bi_], out):
                yield
            pps = out["pp"]
            tmp = tmpr.next()
            for half in range(2):
                self.tt("dve", tmp[:, half * 512:(half + 1) * 512], pps[half][:, :],
                        gt[:, goff + half * 512:goff + (half + 1) * 512], ALU.mult, [pps[half], gt], [tmp])
            res[bi_] = tmp
            yield

        n_t = len(tiles)
        Lq = {}
        for ti in range(min(2, n_t)):
            Lq[ti] = issue_loads(tiles[ti])
        issue_loads_b(tiles[0], Lq[0])
        Pq = {0: {}}
        self.interleave([prep(b_, Lq[0], Pq[0]) for b_ in range(3)])
        for ti, i in enumerate(tiles):
            wv = 1 if i < NCTX else 0
            rows = slice(i * 128, (i + 1) * 128)
            if ti + 2 < n_t:
                Lq[ti + 2] = issue_loads(tiles[ti + 2])
            if ti + 1 < n_t:
                issue_loads_b(tiles[ti + 1], Lq[ti + 1])
            L = Lq.pop(ti)
            Pi = Pq.pop(ti)
            gt = L["gt"]; xt = L["xt"]
            res = {}
            gens = [projb(b_, Pi[b_], gt, res) for b_ in range(3)]
            if ti + 1 < n_t:
                Pq[ti + 1] = {}
                gens += [prep(b_, Lq[ti + 1], Pq[ti + 1]) for b_ in range(3)]
            self.interleave(gens)
            mrg = mrgr.next()
            self.tt("pool", mrg[:], res[0][:], res[1][:], ALU.add, [res[0], res[1]], [mrg])
            self.tt("pool", mrg[:], mrg[:], res[2][:], ALU.add, [mrg, res[2]], [mrg])
            mb16 = ybf.next()
            self.cp("act", mb16[:], mrg[:], [mrg], [mb16])
            out = {}
            for _ in transpose_proj_gen(mb16, None, Wb[3], out):
                pass
            pps = out["pp"]
            G1 = self.G[0][wv]
            tmp = tmpr.next()
            for half in range(2):
                self.tt("dve", tmp[:, half * 512:(half + 1) * 512], pps[half][:, :], G1[:, half * 512:(half + 1) * 512],
                        ALU.mult, [pps[half], G1], [tmp])
            xo = xn.next()
            self.tt("pool", xo[:], tmp[:], xt[:], ALU.add, [tmp, xt], [xo])
            self.stq(S["xs"].ap[rows, :], xo[:], xo, S["xs"], q="act")
        ph.close()
        self.wph.close()

    def ph_ffn(self, l):
        I, S, P = self.I, self.S, self.P
        last = (l == NL - 1)
        tiles = [i for i in range(NT) if not (last and i < NCTX)]
        tok0 = tiles[0] * 128
        ntok = len(tiles) * 128
        wph2 = self.phase()
        wfo = wph2.sb("wfo", [128, 22, D], BF16)
        hph = self.phase()
        hT2 = hph.sb("hT2", [128, 8, ntok], BF16)
        ph = self.phase()
        self.norm_to_hT(ph, hT2, self.A2, 24, tiles, tok0)
        ph.close()
        ph = self.phase()
        stg = ph.rot("sb", "wstg", [128, 8, 512], F32, 3)
        wbr = ph.rot("sb", "wbf", [128, 8, 512], BF16, 4)
        psg = ph.rot("ps", "psg", [128, 512], F32, 3)
        psu = ph.rot("ps", "psu", [128, 512], F32, 3)
        sgr = ph.rot("sb", "sg", [128, 512], F32, 3)
        aor = ph.rot("sb", "ao", [128, ntok], BF16, 2)
        w_f = I["w_ffn_in"][l]
        tgs = [(t0, min(512, ntok - t0)) for t0 in range(0, ntok, 512)]
        def bgf():
            wv_ = I["w_ffn_out"][l].rearrange("(k p) n -> p k n", p=128)
            for c0_ in range(0, D, 512):
                for k0 in range(0, 22, 8):
                    kn = min(8, 22 - k0)
                    st = stg.next()
                    self.ld(st[:, 0:kn, :], wv_[:, k0:k0 + kn, c0_:c0_ + 512], None, st)
                    yield
                    yield
                    self.cp("pool", wfo[:, k0:k0 + kn, c0_:c0_ + 512], st[:, 0:kn, :], [st], [wfo])
                    yield

        bgen = bgf()
        fq = {}

        def ldf(c0_):
            nc_ = min(512, DFF - c0_)
            fq[c0_] = (self.load_w_bf(stg, wbr, w_f, c0_, nc_), self.load_w_bf(stg, wbr, w_f, DFF + c0_, nc_))

        ldf(0)
        for c0 in range(0, DFF, 512):
            ncols = min(512, DFF - c0)
            if c0 + 512 < DFF:
                ldf(c0 + 512)
            wg, wu = fq.pop(c0)
            for jj in range(ncols // 128):
                next(bgen, None)
                ao = aor.next()
                for (t0, tn) in tgs:
                    pg = psg.next(); pu = psu.next()
                    for k in range(8):
                        self.mm(pg[:, 0:tn], wg[:, k, jj * 128:(jj + 1) * 128], hT2[:, k, t0:t0 + tn], k == 0, k == 7, [wg, hT2], [pg])
                    for k in range(8):
                        self.mm(pu[:, 0:tn], wu[:, k, jj * 128:(jj + 1) * 128], hT2[:, k, t0:t0 + tn], k == 0, k == 7, [wu, hT2], [pu])
                    sg = sgr.next()
                    self.act(sg[:, 0:tn], pg[:, 0:tn], AF.Silu, [pg], [sg])
                    self.tt("dve", ao[:, t0:t0 + tn], sg[:, 0:tn], pu[:, 0:tn], ALU.mult, [sg, pu], [ao])
                r0 = c0 + jj * 128
                self.stq(S["aT"].ap[r0:r0 + 128, tok0:tok0 + ntok], ao[:], ao, S["aT"])
        for _ in bgen:
            pass
        ph.close()
        hph.close()
        ph = self.phase()
        abr = ph.rot("sb", "ab", [128, 22, 512], BF16, 2)
        xtr = ph.rot("sb", "xt", [128, D], F32, 3)
        tmp = ph.rot("sb", "tmp", [128, D], F32, 2)
        xnr = ph.rot("sb", "xn", [128, D], F32, 2)
        outr = ph.rot("sb", "oo", [128, D], F32, 2)
        junk = ph.sb("junk", [128, D], F32)
        ssr = ph.rot("sb", "ss", [128, 2], F32, 3)
        ppr = ph.rot("ps", "pp", [128, 512], F32, 4)
        if last:
            fw = self.row_bc(ph, "fw", I["fnw"][0], D)
        abq = {}

        def ld_ab(bi_):
            t0_, tn_ = tgs[bi_]
            ab_ = abr.next()
            self.ld(ab_[:, :, 0:tn_], S["aT"].ap.rearrange("(j p) t -> p j t", p=128)[:, :, tok0 + t0_:tok0 + t0_ + tn_], S["aT"], ab_)
            abq[bi_] = ab_

        ld_ab(0)
        for bi_, (t0, tn) in enumerate(tgs):
            if bi_ + 1 < len(tgs):
                ld_ab(bi_ + 1)
            ab = abq.pop(bi_)
            for tt_ in range(tn // 128):
                i = (tok0 + t0) // 128 + tt_
                wv = 1 if i < NCTX else 0
                rows = slice(i * 128, (i + 1) * 128)
                xt = xtr.next()
                self.ld(xt[:], S["xs"].ap[rows, :], S["xs"], xt)
                G2 = self.G[1][wv]
                tm_ = tmp.next()
                for half in range(2):
                    pp = ppr.next()
                    for j in range(22):
                        self.mm(pp[:, :], ab[:, j, tt_ * 128:(tt_ + 1) * 128], wfo[:, j, half * 512:(half + 1) * 512],
                                j == 0, j == 21, [ab, wfo], [pp])
                    self.tt("dve", tm_[:, half * 512:(half + 1) * 512], pp[:, :], G2[:, half * 512:(half + 1) * 512],
                            ALU.mult, [pp, G2], [tm_])
                xo = xnr.next()
                self.tt("pool", xo[:], tm_[:], xt[:], ALU.add, [tm_, xt], [xo])
                if not last:
                    self.stq(S["xs"].ap[rows, :], xo[:], xo, S["xs"])
                else:
                    ss = ssr.next()
                    self.memset("pool", ss[:], 0.0, [ss])
                    self.act(junk[:], xo[:], AF.Square, [xo, ss], [junk, ss], accum_out=ss[:, 0:1])
                    self.act(ss[:, 1:2], ss[:, 0:1], AF.Ln, [ss], [ss], scale=1.0 / D, bias=EPS)
                    self.act(ss[:, 1:2], ss[:, 1:2], AF.Exp, [ss], [ss], scale=-0.5)
                    oo = outr.next()
                    self.stt("dve", oo[:], xo[:], ss[:, 1:2], fw[:], ALU.mult, ALU.mult, [xo, ss, fw], [oo])
                    self.stq(self.out_ap[(i - NCTX) * 128:(i - NCTX + 1) * 128, :], oo[:], oo, self.outb)
        ph.close()
        wph2.close()


def _consts():
    i = np.arange(128)[:, None]
    j = np.arange(128)[None, :]
    c = np.zeros((128, 6, 128), np.float32)
    c[:, C_ID] = (i == j)
    c[:, C_U] = (i <= j)
    c[:, C_L] = (i >= j)
    c[:, C_LS] = (i > j)
    c[:, C_US] = (i < j)
    c[:, C_ONE] = 1.0
    sel = np.zeros((2, 2, 128), np.float32)
    sel[0, 0] = 1.0
    sel[1, 1] = 1.0
    return c, sel


def _tl(v):
    L = v.shape[0]
    return np.ascontiguousarray(v.reshape(L, -1, 128).transpose(0, 2, 1))


def make_in_maps(inp):
    f = lambda a: np.ascontiguousarray(np.asarray(a, dtype=np.float32))
    consts, sel = _consts()
    shared = {
        "w_mod": f(inp["w_mod"]), "b_modT": _tl(f(inp["b_mod"])), "b_mod": f(inp["b_mod"]),
        "nmwT": _tl(f(inp["norm_mix_w"])), "nfwT": _tl(f(inp["norm_ffn_w"])),
        "fnw": f(inp["final_norm_w"]).reshape(1, D),
        "w_in": f(inp["w_in"]),
        "sconvT": np.ascontiguousarray(f(inp["ssd_conv_w"]).reshape(NL, 5, 10, 128).transpose(0, 3, 2, 1)),
        "sconvb": _tl(f(inp["ssd_conv_b"])),
        "mconvT": np.ascontiguousarray(f(inp["ml_conv_w"]).reshape(NL, 5, 8, 128).transpose(0, 3, 2, 1)),
        "mconvb": _tl(f(inp["ml_conv_b"])),
        "sdtb": f(inp["ssd_dt_bias"]).reshape(NL, 32), "salog": f(inp["ssd_a_log"]).reshape(NL, 32),
        "sd": f(inp["ssd_d"]), "snwT": _tl(f(inp["ssd_norm_w"])),
        "mib": f(inp["ml_i_bias"]).reshape(NL, 8), "mfb": f(inp["ml_f_bias"]).reshape(NL, 8),
        "mnwT": _tl(f(inp["ml_norm_w"])),
        "aup": np.ascontiguousarray(np.concatenate([f(inp["gla_a_up"]), f(inp["gla_a_bias"])[:, :, None, :]], axis=2)),
        "gnwT": _tl(f(inp["gla_norm_w"])),
        "w_b_ssd": f(inp["w_b_ssd"]), "w_b_ml": f(inp["w_b_ml"]), "w_b_gla": f(inp["w_b_gla"]), "w_out": f(inp["w_out"]),
        "w_ffn_in": f(inp["w_ffn_in"]), "w_ffn_out": f(inp["w_ffn_out"]),
        "consts": consts, "sel": sel,
    }
    x = f(inp["x"]); c = f(inp["c"]); ctx = f(inp["ctx"]); cc = f(inp["c_ctx"])
    maps = []
    for b in range(x.shape[0]):
        cvec = np.stack([c[b].reshape(8, 128).T, cc.reshape(8, 128).T], axis=-1)
        m = dict(shared)
        m["x"] = np.ascontiguousarray(x[b])
        m["ctx"] = np.ascontiguousarray(ctx[b])
        m["cvec"] = np.ascontiguousarray(cvec.astype(np.float32))
        maps.append(m)
    return maps


_NC_CACHE = {}
FUSED = True


def kernel(**inputs):
    maps = make_in_maps(inputs)
    if FUSED:
        if "nc" not in _NC_CACHE:
            _NC_CACHE["nc"] = Builder().build()
        res = run_bass_kernel_spmd(_NC_CACHE["nc"], maps, core_ids=list(range(8)))
        return np.stack([np.asarray(r["out"], dtype=np.float32) for r in res.results], axis=0)
    if "l0" not in _NC_CACHE:
        _NC_CACHE["l0"] = Builder(debug={"xs"}, layer_ids=[0]).build()
        _NC_CACHE["l1"] = Builder(layer_ids=[1]).build()
    res = run_bass_kernel_spmd(_NC_CACHE["l0"], maps, core_ids=list(range(8)))
    for m, r in zip(maps, res.results):
        xs = np.asarray(r["xs"], dtype=np.float32)
        m["ctx"] = np.ascontiguousarray(xs[:256])
        m["x"] = np.ascontiguousarray(xs[256:])
    res = run_bass_kernel_spmd(_NC_CACHE["l1"], maps, core_ids=list(range(8)))
    return np.stack([np.asarray(r["out"], dtype=np.float32) for r in res.results], axis=0)
```
